# Optimizing a Trainium2 kernel written in Bass

```python
import jax, jax.numpy as jnp
from jax import lax
import numpy as np

D_MODEL = 2048
BATCH = 4
SEQ = 4096
DEPTH = 2
DEC_BATCH = 8
DEC_SEQ = 2048
PAST_LEN = 128

N_HEADS = 16
QK_NOPE_DIM = 128
QK_ROPE_DIM = 64
V_HEAD_DIM = 128
Q_LORA_RANK = D_MODEL // 4
KV_LORA_RANK = D_MODEL // 4
ROPE_THETA = 10000.0
Q_BLOCK = 128
FOURIER_WIDTH = D_MODEL // 2
N_FOURIER_GROUPS = 4
FOURIER_GROUP_DIM = FOURIER_WIDTH // N_FOURIER_GROUPS
N_BRANCHES = 2
D_FF = 11 * D_MODEL // 4
NORM_EPS = 1e-6
SPLITS = (FOURIER_WIDTH,
          FOURIER_WIDTH + Q_LORA_RANK,
          FOURIER_WIDTH + Q_LORA_RANK + KV_LORA_RANK,
          FOURIER_WIDTH + Q_LORA_RANK + KV_LORA_RANK + QK_ROPE_DIM)
D_IN = SPLITS[-1] + N_BRANCHES * D_MODEL

kernel_name = "hybrid_fnet_mla_macaron_encoder"


def rmsnorm(x, g):
    xf = x.astype(jnp.float32)
    y = xf * lax.rsqrt(jnp.mean(xf * xf, axis=-1, keepdims=True) + NORM_EPS)
    return y.astype(x.dtype) * g


def swiglu(h, w_gate, w_up, w_down):
    return (jax.nn.silu(h @ w_gate) * (h @ w_up)) @ w_down


def rope_tables(seq):
    half = QK_ROPE_DIM // 2
    inv_freq = 1.0 / (ROPE_THETA ** (jnp.arange(half, dtype=jnp.float32) / half))
    ang = jnp.arange(seq, dtype=jnp.float32)[:, None] * inv_freq[None, :]
    return jnp.cos(ang), jnp.sin(ang)


def apply_rope(x, cos, sin):
    x1, x2 = jnp.split(x.astype(jnp.float32), 2, axis=-1)
    out = jnp.concatenate([x1 * cos - x2 * sin, x1 * sin + x2 * cos], axis=-1)
    return out.astype(x.dtype)


def mla_attention(q_nope, q_rope, k_nope, k_rope, v):
    b, s, h, _ = q_nope.shape
    nblk = s // Q_BLOCK
    scale = (QK_NOPE_DIM + QK_ROPE_DIM) ** -0.5

    def to_blocks(t):
        return jnp.moveaxis(t.reshape(b, nblk, Q_BLOCK, *t.shape[2:]), 1, 0)

    def block(qs):
        qn, qr = qs
        sc = (jnp.einsum('bqhd,bkhd->bhqk', qn, k_nope)
              + jnp.einsum('bqhr,bkr->bhqk', qr, k_rope))
        p = jax.nn.softmax(sc.astype(jnp.float32) * scale, axis=-1).astype(v.dtype)
        return jnp.einsum('bhqk,bkhd->bqhd', p, v)

    out = lax.map(block, (to_blocks(q_nope), to_blocks(q_rope)))
    return jnp.moveaxis(out, 0, 1).reshape(b, s, h * V_HEAD_DIM)


def token_mixer(h, w_in, b_gate, q_a_norm, kv_a_norm, w_uq, w_ukv, w_fourier, w_mla_o, w_out, cos, sin):
    b, s, _ = h.shape
    proj = h @ w_in
    u_f, c_q, c_kv, k_r, gate_pre = jnp.split(proj, SPLITS, axis=-1)

    uf = u_f.reshape(b, s, N_FOURIER_GROUPS, FOURIER_GROUP_DIM).astype(jnp.float32)
    mixed = jnp.fft.fft2(uf, axes=(1, 3), norm="ortho").real.astype(h.dtype)
    y_a = mixed.reshape(b, s, FOURIER_WIDTH) @ w_fourier

    q = (rmsnorm(c_q, q_a_norm) @ w_uq).reshape(b, s, N_HEADS, QK_NOPE_DIM + QK_ROPE_DIM)
    q_nope, q_rope = jnp.split(q, [QK_NOPE_DIM], axis=-1)
    kv = (rmsnorm(c_kv, kv_a_norm) @ w_ukv).reshape(b, s, N_HEADS, QK_NOPE_DIM + V_HEAD_DIM)
    k_nope, v = jnp.split(kv, [QK_NOPE_DIM], axis=-1)
    q_rope = apply_rope(q_rope, cos[:, None, :], sin[:, None, :])
    k_rope = apply_rope(k_r, cos, sin)
    y_b = mla_attention(q_nope, q_rope, k_nope, k_rope, v) @ w_mla_o

    gates = jax.nn.sigmoid((gate_pre + b_gate).astype(jnp.float32)).astype(h.dtype)
    g_a, g_b = jnp.split(gates, N_BRANCHES, axis=-1)
    return (g_a * y_a + g_b * y_b) @ w_out


def trunk(x, ffn1_norm, ffn1_w_gate, ffn1_w_up, ffn1_w_down, mix_norm, w_in, b_gate,
          q_a_norm, kv_a_norm, w_uq, w_ukv, w_fourier, w_mla_o, w_out,
          ffn2_norm, ffn2_w_gate, ffn2_w_up, ffn2_w_down, final_norm):
    cos, sin = rope_tables(x.shape[1])
    for l in range(DEPTH):
        x = x + 0.5 * swiglu(rmsnorm(x, ffn1_norm[l]), ffn1_w_gate[l], ffn1_w_up[l], ffn1_w_down[l])
        x = x + token_mixer(rmsnorm(x, mix_norm[l]), w_in[l], b_gate[l], q_a_norm[l], kv_a_norm[l],
                            w_uq[l], w_ukv[l], w_fourier[l], w_mla_o[l], w_out[l], cos, sin)
        x = x + 0.5 * swiglu(rmsnorm(x, ffn2_norm[l]), ffn2_w_gate[l], ffn2_w_up[l], ffn2_w_down[l])
    return rmsnorm(x, final_norm)


def setup_inputs(seed: int = 0) -> dict:
    key = jax.random.key(seed)
    ks = jax.random.split(key, 24)

    def w(k, shape, fan_in):
        return jax.random.normal(k, shape, jnp.float32) * (fan_in ** -0.5)

    def gain(k, shape):
        return 1.0 + 0.02 * jax.random.normal(k, shape, jnp.float32)

    L = DEPTH
    return {
        "x_prompt": jax.random.normal(ks[0], (BATCH, SEQ, D_MODEL), jnp.float32),
        "x_sample": jax.random.normal(ks[1], (DEC_BATCH, DEC_SEQ, D_MODEL), jnp.float32),
        "ffn1_norm": gain(ks[2], (L, D_MODEL)),
        "ffn1_w_gate": w(ks[3], (L, D_MODEL, D_FF), D_MODEL),
        "ffn1_w_up": w(ks[4], (L, D_MODEL, D_FF), D_MODEL),
        "ffn1_w_down": w(ks[5], (L, D_FF, D_MODEL), D_FF),
        "mix_norm": gain(ks[6], (L, D_MODEL)),
        "w_in": w(ks[7], (L, D_MODEL, D_IN), D_MODEL),
        "b_gate": 0.01 * jax.random.normal(ks[8], (L, N_BRANCHES * D_MODEL), jnp.float32),
        "q_a_norm": gain(ks[9], (L, Q_LORA_RANK)),
        "kv_a_norm": gain(ks[10], (L, KV_LORA_RANK)),
        "w_uq": w(ks[11], (L, Q_LORA_RANK, N_HEADS * (QK_NOPE_DIM + QK_ROPE_DIM)), Q_LORA_RANK),
        "w_ukv": w(ks[12], (L, KV_LORA_RANK, N_HEADS * (QK_NOPE_DIM + V_HEAD_DIM)), KV_LORA_RANK),
        "w_fourier": w(ks[13], (L, FOURIER_WIDTH, D_MODEL), FOURIER_WIDTH),
        "w_mla_o": w(ks[14], (L, N_HEADS * V_HEAD_DIM, D_MODEL), N_HEADS * V_HEAD_DIM),
        "w_out": w(ks[15], (L, D_MODEL, D_MODEL), D_MODEL),
        "ffn2_norm": gain(ks[16], (L, D_MODEL)),
        "ffn2_w_gate": w(ks[17], (L, D_MODEL, D_FF), D_MODEL),
        "ffn2_w_up": w(ks[18], (L, D_MODEL, D_FF), D_MODEL),
        "ffn2_w_down": w(ks[19], (L, D_FF, D_MODEL), D_FF),
        "final_norm": gain(ks[20], (D_MODEL,)),
    }


def reference(x_prompt, x_sample, ffn1_norm, ffn1_w_gate, ffn1_w_up, ffn1_w_down, mix_norm, w_in, b_gate,
              q_a_norm, kv_a_norm, w_uq, w_ukv, w_fourier, w_mla_o, w_out,
              ffn2_norm, ffn2_w_gate, ffn2_w_up, ffn2_w_down, final_norm):
    y_prompt = trunk(x_prompt, ffn1_norm, ffn1_w_gate, ffn1_w_up, ffn1_w_down, mix_norm, w_in, b_gate,
                     q_a_norm, kv_a_norm, w_uq, w_ukv, w_fourier, w_mla_o, w_out,
                     ffn2_norm, ffn2_w_gate, ffn2_w_up, ffn2_w_down, final_norm)
    y_sample = trunk(x_sample, ffn1_norm, ffn1_w_gate, ffn1_w_up, ffn1_w_down, mix_norm, w_in, b_gate,
                     q_a_norm, kv_a_norm, w_uq, w_ukv, w_fourier, w_mla_o, w_out,
                     ffn2_norm, ffn2_w_gate, ffn2_w_up, ffn2_w_down, final_norm)
    return (y_prompt, y_sample)
```

```python
import numpy as np
from contextlib import ExitStack
import concourse.bass as bass
import concourse.mybir as mybir
from concourse.bass_utils import run_bass_kernel_spmd

F32 = mybir.dt.float32
BF16 = mybir.dt.bfloat16
AF = mybir.ActivationFunctionType
ALU = mybir.AluOpType

NORM_EPS = 1e-6
ROPE_THETA = 10000.0
BIGMASK = 4096.0


class Cfg:
    def __init__(s, D=2048, DFF=5632, NH=16, QL=512, KVL=512, FW=1024, GD=256, NTOK=4096, L=2, T=512):
        s.D, s.DFF, s.NH, s.QL, s.KVL, s.FW, s.GD, s.NTOK, s.L, s.T = D, DFF, NH, QL, KVL, FW, GD, NTOK, L, T
        s.KC = D // 128
        s.FC = DFF // 128
        s.NT = NTOK // T
        s.NS = T // 128
        s.QC = QL // 128
        s.KVC = KVL // 128
        s.FWC = FW // 128
        s.HC = NH
        s.NKC = NTOK // 128
        s.DIN = FW + QL + KVL + 64 + 2 * D
        s.c_cq = FW
        s.c_ckv = FW + QL
        s.c_kr = FW + QL + KVL
        s.c_g = s.c_kr + 64
        s.vec_off = {}
        o = 0
        for l in range(L):
            for nm, n in (("ffn1", s.KC), ("mix", s.KC), ("ffn2", s.KC), ("qa", s.QC), ("kva", s.KVC),
                          ("bg", 2 * s.KC)):
                s.vec_off[(nm, l)] = o
                o += n
        s.vec_off[("final", 0)] = o
        o += s.KC
        s.NVEC = o


class Op:
    __slots__ = ("eng", "fn", "deps", "dma", "dsem", "dval", "sig", "sigval")


class Prog:
    ENGS = ("sp", "pool", "pe", "act", "dve")

    def __init__(self, n_sp=40, n_pool=24):
        self.ops = []
        self.lw = {}
        self.rd = {}
        self.dma_n = {"sp": n_sp, "pool": n_pool}
        self.dma_rr = {"sp": 0, "pool": 0}
        self.dma_last = {"sp": [None] * n_sp, "pool": [None] * n_pool}
        self.dma_cnt = {"sp": [0] * n_sp, "pool": [0] * n_pool}

    def add(self, eng, fn, reads=(), writes=(), dma=False):
        deps = set()
        lw, rd = self.lw, self.rd
        for k in reads:
            j = lw.get(k)
            if j is not None:
                deps.add(j)
        for k in writes:
            j = lw.get(k)
            if j is not None:
                deps.add(j)
            r = rd.get(k)
            if r:
                deps.update(r)
        idx = len(self.ops)
        op = Op()
        op.eng, op.fn, op.dma, op.sig, op.sigval = eng, fn, dma, False, 0
        op.dsem = op.dval = None
        if dma:
            s = self.dma_rr[eng]
            self.dma_rr[eng] = (s + 1) % self.dma_n[eng]
            prev = self.dma_last[eng][s]
            if prev is not None:
                deps.add(prev)
            self.dma_last[eng][s] = idx
            self.dma_cnt[eng][s] += 1
            op.dsem = (eng, s)
            op.dval = 16 * self.dma_cnt[eng][s]
        ops = self.ops
        if eng == "pe":
            deps = {j for j in deps if ops[j].dma or ops[j].eng != "pe"}
        op.deps = deps
        for j in deps:
            if not ops[j].dma:
                ops[j].sig = True
        for k in reads:
            rd.setdefault(k, []).append(idx)
        for k in writes:
            lw[k] = idx
            rd[k] = []
        ops.append(op)
        return idx

    def emit(self, nc, block, sems, dsems):
        ops = self.ops
        cnt = {e: 0 for e in self.ENGS}
        for op in ops:
            if op.sig:
                cnt[op.eng] += 1
                op.sigval = cnt[op.eng]
        by_eng = {e: [] for e in self.ENGS}
        for op in ops:
            by_eng[op.eng].append(op)

        def run(e, name):
            waited = {}
            for op in by_eng[name]:
                need = {}
                for j in op.deps:
                    d = ops[j]
                    if d.dma:
                        key, val = ("d",) + d.dsem, d.dval
                    else:
                        key, val = ("c", d.eng), d.sigval
                    if val > need.get(key, 0):
                        need[key] = val
                for key, val in need.items():
                    if waited.get(key, 0) >= val:
                        continue
                    waited[key] = val
                    sem = dsems[key[1]][key[2]] if key[0] == "d" else sems[key[1]]
                    e.wait_ge(sem, val)
                ins = op.fn(e)
                if op.dma:
                    ins.then_inc(dsems[op.dsem[0]][op.dsem[1]], 16)
                elif op.sig:
                    ins.then_inc(sems[name], 1)
            if name in ("sp", "pool"):
                for s in range(self.dma_n[name]):
                    c = self.dma_cnt[name][s]
                    if c and waited.get(("d", name, s), 0) < 16 * c:
                        e.wait_ge(dsems[name][s], 16 * c)

        @block.sync
        def _(e):
            run(e, "sp")

        @block.gpsimd
        def _(e):
            run(e, "pool")

        @block.tensor
        def _(e):
            run(e, "pe")

        @block.scalar
        def _(e):
            run(e, "act")

        @block.vector
        def _(e):
            run(e, "dve")


class Reg:
    __slots__ = ("arena", "off", "n", "keys")

    def __init__(self, arena, off, n):
        assert off % 4 == 0 and n % 4 == 0
        self.arena, self.off, self.n = arena, off, n
        self.keys = tuple(("sb", b) for b in range(off // 512, (off + n - 1) // 512 + 1))

    def sub(self, off, n):
        assert off + n <= self.n, (off, n, self.n)
        return Reg(self.arena, self.off + off, n)

    def f32(self, *shape):
        ap = self.arena[:, self.off // 4:(self.off + self.n) // 4]
        return _shape(ap, shape)

    def bf(self, *shape):
        ap = self.arena[:, self.off // 4:(self.off + self.n) // 4].bitcast(BF16)
        return _shape(ap, shape)


def _shape(ap, shape):
    if len(shape) <= 1:
        return ap
    if len(shape) == 2:
        return ap.rearrange("p (a b) -> p a b", a=shape[0])
    if len(shape) == 3:
        return ap.rearrange("p (a b c) -> p a b c", a=shape[0], b=shape[1])
    if len(shape) == 4:
        return ap.rearrange("p (a b c d) -> p a b c d", a=shape[0], b=shape[1], c=shape[2])
    raise ValueError(shape)


class Alloc:
    def __init__(self, arena, base, limit):
        self.arena, self.o, self.limit = arena, base, limit

    def get(self, n):
        r = Reg(self.arena, self.o, n)
        self.o += (n + 511) // 512 * 512
        assert self.o <= self.limit, ("SBUF overflow", self.o, self.limit)
        return r


def build(cfg):
    c = cfg
    D, DFF, NH, QL, KVL, FW, GD, NTOK, L, T = c.D, c.DFF, c.NH, c.QL, c.KVL, c.FW, c.GD, c.NTOK, c.L, c.T
    KC, FC, NT, NS, QC, KVC, FWC, NKC = c.KC, c.FC, c.NT, c.NS, c.QC, c.KVC, c.FWC, c.NKC
    assert T == 512 and DFF % 256 == 0 and D % 512 == 0 and NH % 4 == 0 and FW % 512 == 0 and GD == 256
    nc = bass.Bass("TRN2", target_bir_lowering=False)
    P = Prog()

    def din(name, shape, dt=F32):
        return nc.dram_tensor(name, list(shape), dt, kind="ExternalInput").ap()

    def dscr(name, shape, dt=BF16):
        return nc.dram_tensor(name, list(shape), dt, kind="Internal").ap()

    x_in = din("x", [NTOK, D])
    w_gate = [din("ffn1_w_gate", [L, D, DFF]), din("ffn2_w_gate", [L, D, DFF])]
    w_up = [din("ffn1_w_up", [L, D, DFF]), din("ffn2_w_up", [L, D, DFF])]
    w_down = [din("ffn1_w_down", [L, DFF, D]), din("ffn2_w_down", [L, DFF, D])]
    w_in = din("w_in", [L, D, c.DIN])
    w_uq = din("w_uq", [L, QL, NH * 192])
    w_ukv = din("w_ukv", [L, KVL, NH * 256])
    w_fourier = din("w_fourier", [L, FW, D])
    w_mla_o = din("w_mla_o", [L, NH * 128, D])
    w_out = din("w_out", [L, D, D])
    vecs_in = din("vecs", [128, c.NVEC])
    gfin_in = din("ident32", [128, 128])
    rope_c_in = din("rope_c", [64, NTOK])
    rope_s_in = din("rope_s", [64, NTOK])
    flags_in = din("flags", [4, NTOK])
    dft_c_in = din("dft_c", [NTOK, NTOK])
    dft_s_in = din("dft_s", [NTOK, NTOK])
    cdft_in = din("cdft", [2, GD, GD])
    y_out = nc.dram_tensor("y", [NTOK, D], F32, kind="ExternalOutput").ap()

    NG_FF = DFF // 256
    NG_WD = D // 256
    WGU = [[dscr(f"WGU{l}_{f}", [NG_FF, 128, 2, 2, KC, 128]) for f in range(2)] for l in range(L)]
    WD = [[dscr(f"WD{l}_{f}", [NG_WD, 128, 2, FC, 128]) for f in range(2)] for l in range(L)]
    WUF = [dscr(f"WUF{l}", [FW // 512, 128, KC, 512]) for l in range(L)]
    WCQ = [dscr(f"WCQ{l}", [128, QC, KC, 128]) for l in range(L)]
    WCKV = [dscr(f"WCKV{l}", [128, KVC, KC, 128]) for l in range(L)]
    WKR = [dscr(f"WKR{l}", [128, 2, KC, 64]) for l in range(L)]
    WGATE = [dscr(f"WGATE{l}", [D // 256, 128, 2, 2, KC, 128]) for l in range(L)]
    WOUT = [dscr(f"WOUT{l}", [D // 512, 128, 4, KC, 128]) for l in range(L)]
    WMO = [dscr(f"WMO{l}", [D // 512, 128, 4, NH, 128]) for l in range(L)]
    WFO = [dscr(f"WFO{l}", [D // 512, 128, 4, FWC, 128]) for l in range(L)]
    WUQ = [dscr(f"WUQ{l}", [NH // 4, 128, 4, QC * 128 + 2 * QC * 64]) for l in range(L)]
    WUK = [dscr(f"WUK{l}", [NH // 4, 128, 4, KVC, 128]) for l in range(L)]
    WUV = [dscr(f"WUV{l}", [NH // 4, 128, KVC, 512]) for l in range(L)]
    DFT = dscr("DFT", [NT, NKC // 4, 128, 4, 2, 512])
    CDFT = dscr("CDFT", [128, 2, 2, GD])
    QF = dscr("QF", [2, NTOK])
    X1 = [dscr(f"X1_{l}", [NT, 128, KC * T], F32) for l in range(L)]
    UF = [dscr(f"UF{l}", [FWC, 128, NKC, 128]) for l in range(L)]
    KT = [dscr(f"KT{l}", [NH, 128, NTOK]) for l in range(L)]
    VS = [dscr(f"VS{l}", [NH, 128, NKC, 128]) for l in range(L)]
    KR = [dscr(f"KR{l}", [66, NTOK]) for l in range(L)]

    es = ExitStack()
    with es:
        ARENA_BYTES = 200 * 1024
        arena = es.enter_context(nc.sbuf_tensor("arena", [128, ARENA_BYTES // 4], F32))
        banks = [es.enter_context(nc.psum_tensor(f"bank{i}", [128, 512], F32)) for i in range(8)]
        sems = {e: es.enter_context(nc.semaphore(f"s_{e}")) for e in ("pool", "pe", "act", "dve")}
        dsems = {q: [es.enter_context(nc.semaphore(f"d_{q}{i}")) for i in range(P.dma_n[q])]
                 for q in ("sp", "pool")}
        block = es.enter_context(nc.Block())

        def bk(i):
            return (("ps", i),)

        A = Alloc(arena, 0, ARENA_BYTES)
        r_xT = A.get(KC * T * 4)
        r_hT = A.get(KC * T * 2)
        r_vecs = A.get(c.NVEC * 4)
        r_id32 = A.get(128 * 4)
        r_ones = A.get(128 * 2)
        r_ones32 = A.get(128 * 4)
        r_eps = A.get(512)
        r_cdft = A.get(2 * 2 * GD * 2)
        r_sq = [A.get(T * 2) for _ in range(4)]
        r_std = A.get(T * 4)
        r_rstd = A.get(T * 4)
        r_sg = [A.get(T * 4) for _ in range(2)]
        r_rope = A.get(2 * T * 4)
        r_tmp = [A.get(T * 4) for _ in range(2)]
        R0 = A.o
        RSIZE = ARENA_BYTES - R0
        r_R = Reg(arena, R0, RSIZE)

        xT = [r_xT.sub(k * T * 4, T * 4) for k in range(KC)]
        hT = [r_hT.sub(k * T * 2, T * 2) for k in range(KC)]
        vecs = r_vecs.f32()
        id32 = r_id32.f32()
        ones_bf = r_ones.bf()
        eps_ap = r_eps.f32()[:, 0:1]

        def vcol(nm, l, k):
            o = c.vec_off[(nm, l)] + k
            return vecs[:, o:o + 1]

        def dma(q, out, in_, reads, writes):
            eng = "sp" if q == "sp" else "pool"
            return P.add(eng, lambda e: e.dma_start(out=out, in_=in_), reads, writes, dma=True)

        wsub = {}

        def conv(dst, src, wkey):
            lst = wsub.setdefault(wkey, [])
            sk = wkey + ("#", len(lst))
            lst.append(sk)
            dma("pool", dst, src, (), (sk,))

        def WK(*wkey):
            return tuple(wsub[wkey])

        def mm(bank, lhsT, rhs, start, stop, reads, m=128, n=512):
            out = banks[bank][0:m, 0:n]
            P.add("pe", lambda e: e.matmul(out, lhsT=lhsT, rhs=rhs, start=start, stop=stop),
                  reads, bk(bank))

        def tr(bank, col, in_ap, reads):
            out = banks[bank][:, col:col + 128]
            P.add("pe", lambda e: e.transpose(out, in_ap, id32), tuple(reads) + r_id32.keys, bk(bank))

        bank_rr = {}

        def nb(cls, lst):
            i = bank_rr.get(cls, 0)
            bank_rr[cls] = i + 1
            return lst[i % len(lst)]

        dma("sp", vecs, vecs_in, (), r_vecs.keys)
        dma("sp", id32, gfin_in, (), r_id32.keys)
        P.add("dve", lambda e: e.memset(ones_bf, 1.0), (), r_ones.keys)
        ones32 = r_ones32.f32()
        P.add("dve", lambda e: e.memset(ones32, 1.0), (), r_ones32.keys)
        P.add("dve", lambda e: e.memset(r_eps.f32(), NORM_EPS), (), r_eps.keys)

        def jobs_ffn(l, f):
            jobs = []
            wg = w_gate[f][l].rearrange("(k p) n -> p k n", p=128)
            wu = w_up[f][l].rearrange("(k p) n -> p k n", p=128)
            wd = w_down[f][l].rearrange("(k p) n -> p k n", p=128)
            for g in range(NG_FF):
                for gu, w in ((0, wg), (1, wu)):
                    for b in range(2):
                        col = g * 256 + b * 128
                        jobs.append((WGU[l][f][g, :, gu, b, :, :], w[:, :, col:col + 128], ("WGU", l, f, g)))
            for g in range(NG_WD):
                for b in range(2):
                    col = g * 256 + b * 128
                    jobs.append((WD[l][f][g, :, b, :, :], wd[:, :, col:col + 128], ("WD", l, f, g)))
            return jobs

        def jobs_projA(l):
            jobs = []
            wi = w_in[l].rearrange("(k p) n -> p k n", p=128)
            for g in range(FW // 512):
                jobs.append((WUF[l][g], wi[:, :, g * 512:(g + 1) * 512], ("WUF", l, g)))
            for b in range(KVC):
                col = c.c_ckv + b * 128
                jobs.append((WCKV[l][:, b, :, :], wi[:, :, col:col + 128], ("WCKV", l)))
            jobs.append((WKR[l][:, 0, :, :], wi[:, :, c.c_kr:c.c_kr + 64], ("WKR", l)))
            jobs.append((WKR[l][:, 1, :, 0:32], wi[:, :, c.c_kr + 32:c.c_kr + 64], ("WKR", l)))
            jobs.append((WKR[l][:, 1, :, 32:64], wi[:, :, c.c_kr:c.c_kr + 32], ("WKR", l)))
            wkv = w_ukv[l].rearrange("(k p) n -> p k n", p=128)
            for g in range(NH // 4):
                for hh in range(4):
                    h = g * 4 + hh
                    jobs.append((WUK[l][g, :, hh, :, :], wkv[:, :, h * 256:h * 256 + 128], ("WUK", l, g)))
                    jobs.append((WUV[l][g, :, :, hh * 128:(hh + 1) * 128],
                                 wkv[:, :, h * 256 + 128:h * 256 + 256], ("WUV", l, g)))
            return jobs

        def jobs_mix(l):
            jobs = []
            wi = w_in[l].rearrange("(k p) n -> p k n", p=128)
            for b in range(QC):
                col = c.c_cq + b * 128
                jobs.append((WCQ[l][:, b, :, :], wi[:, :, col:col + 128], ("WCQ", l)))
            for g in range(D // 256):
                for ab in range(2):
                    for b in range(2):
                        col = c.c_g + ab * D + g * 256 + b * 128
                        jobs.append((WGATE[l][g, :, ab, b, :, :], wi[:, :, col:col + 128], ("WGATE", l, g)))
            wo = w_out[l].rearrange("(k p) n -> p k n", p=128)
            wm = w_mla_o[l].rearrange("(k p) n -> p k n", p=128)
            wf = w_fourier[l].rearrange("(k p) n -> p k n", p=128)
            for g in range(D // 512):
                for b in range(4):
                    col = g * 512 + b * 128
                    jobs.append((WOUT[l][g, :, b, :, :], wo[:, :, col:col + 128], ("WOUT", l, g)))
                    jobs.append((WMO[l][g, :, b, :, :], wm[:, :, col:col + 128], ("WMO", l, g)))
                    jobs.append((WFO[l][g, :, b, :, :], wf[:, :, col:col + 128], ("WFO", l, g)))
            wq = w_uq[l].rearrange("(k p) n -> p k n", p=128)
            NQ = QC * 128
            for g in range(NH // 4):
                for hh in range(4):
                    h = g * 4 + hh
                    dst = WUQ[l][g, :, hh, :]
                    jobs.append((dst[:, 0:NQ].rearrange("p (k n) -> p k n", k=QC),
                                 wq[:, :, h * 192:h * 192 + 128], ("WUQ", l, g)))
                    r0 = dst[:, NQ:NQ + QC * 64].rearrange("p (k n) -> p k n", k=QC)
                    r1 = dst[:, NQ + QC * 64:NQ + 2 * QC * 64].rearrange("p (k n) -> p k n", k=QC)
                    jobs.append((r0, wq[:, :, h * 192 + 128:h * 192 + 192], ("WUQ", l, g)))
                    jobs.append((r1[:, :, 0:32], wq[:, :, h * 192 + 160:h * 192 + 192], ("WUQ", l, g)))
                    jobs.append((r1[:, :, 32:64], wq[:, :, h * 192 + 128:h * 192 + 160], ("WUQ", l, g)))
            return jobs

        def jobs_dft():
            jobs = []
            for t in range(NT):
                for ng in range(NKC // 4):
                    for cs, src in ((0, dft_c_in), (1, dft_s_in)):
                        s3 = src[ng * 512:(ng + 1) * 512, t * 512:(t + 1) * 512].rearrange("(j p) n -> p j n", p=128)
                        jobs.append((DFT[t, ng, :, :, cs, :], s3, ("DFT", t, ng)))
            for cs in range(2):
                jobs.append((CDFT[:, cs, :, :], cdft_in[cs].rearrange("(k p) n -> p k n", p=128), ("CDFT",)))
            jobs.append((QF[:, :], flags_in[2:4, :], ("QF",)))
            for l in range(L):
                jobs.append((KR[l][64:66, :], flags_in[0:2, :], ("KRF", l)))
            return jobs

        def run_jobs(jobs):
            for dst, src, key in jobs:
                conv(dst, src, key)

        def spread(jobs, n):
            out, k = [], len(jobs)
            for i in range(n):
                out.append(jobs[i * k // n:(i + 1) * k // n])
            return out

        ffn_act = [r_R.sub(k * T * 2, T * 2) for k in range(FC)]
        o = FC * T * 2
        WGU_SLOT = 2 * 2 * KC * 128 * 2
        ffn_wgu = [r_R.sub(o + i * WGU_SLOT, WGU_SLOT) for i in range(2)]
        o += 2 * WGU_SLOT
        WD_SLOT = 2 * FC * 128 * 2
        ffn_wd = [r_R.sub(o + i * WD_SLOT, WD_SLOT) for i in range(2)]
        o += 2 * WD_SLOT
        assert o <= RSIZE, ("FFN view", o, RSIZE)
        WSLOT = max(4 * KC * 128 * 2, 4 * NH * 128 * 2, KC * 512 * 2)
        wslots = [r_R.sub(i * WSLOT, WSLOT) for i in range(2)]
        ws_rr = [0]

        def wslot():
            i = ws_rr[0]
            ws_rr[0] += 1
            return wslots[i % 2]

        V0 = 2 * WSLOT

        def norm(src, n, dim, gname, l, dst, dst_f32=False):
            bank = nb("stat", [7])
            for k in range(n):
                sq = r_sq[k % 4]
                s_ap, q_ap = src[k].f32(), sq.bf()
                P.add("dve", lambda e, s_ap=s_ap, q_ap=q_ap: e.tensor_tensor(out=q_ap, in0=s_ap, in1=s_ap, op=ALU.mult),
                      src[k].keys, sq.keys)
                mm(bank, ones_bf, q_ap, k == 0, k == n - 1, r_ones.keys + sq.keys)
            std, rstd = r_std.f32(), r_rstd.f32()
            bap = banks[bank][:, :]
            P.add("act", lambda e: e.activation(out=std, in_=bap, func=AF.Sqrt, bias=eps_ap, scale=1.0 / dim),
                  bk(bank) + r_eps.keys, r_std.keys)
            P.add("dve", lambda e: e.reciprocal(out=rstd, in_=std), r_std.keys, r_rstd.keys)
            for k in range(n):
                s_ap = src[k].f32()
                d_ap = dst[k].f32() if dst_f32 else dst[k].bf()
                g_ap = vcol(gname, l, k)
                P.add("dve", lambda e, s_ap=s_ap, d_ap=d_ap, g_ap=g_ap: e.scalar_tensor_tensor(
                    out=d_ap, in0=s_ap, scalar=g_ap, in1=rstd, op0=ALU.mult, op1=ALU.mult),
                    src[k].keys + r_rstd.keys + r_vecs.keys, dst[k].keys)

        def ffn(l, f):
            norm(xT, KC, D, "ffn1" if f == 0 else "ffn2", l, hT)
            hkeys = tuple(k for r in hT for k in r.keys)
            for g in range(NG_FF):
                slot = ffn_wgu[g % 2]
                dma("sp", slot.bf(), WGU[l][f][g].rearrange("p a b k n -> p (a b k n)"),
                    WK("WGU", l, f, g), slot.keys)
                w = slot.bf(2, 2, KC, 128)
                for b in range(2):
                    fb = g * 2 + b
                    bg, bu = nb("g", [0, 1]), nb("u", [2, 3])
                    for gu, bank in ((0, bg), (1, bu)):
                        for k in range(KC):
                            mm(bank, w[:, gu, b, k, :], hT[k].bf(), k == 0, k == KC - 1, slot.keys + hT[k].keys)
                    sg = r_sg[fb % 2]
                    sg_ap = sg.f32()
                    g_ap, u_ap, a_ap = banks[bg][:, :], banks[bu][:, :], ffn_act[fb].bf()
                    P.add("act", lambda e, sg_ap=sg_ap, g_ap=g_ap: e.activation(out=sg_ap, in_=g_ap, func=AF.Silu),
                          bk(bg), sg.keys)
                    P.add("dve", lambda e, sg_ap=sg_ap, u_ap=u_ap, a_ap=a_ap: e.tensor_tensor(
                        out=a_ap, in0=sg_ap, in1=u_ap, op=ALU.mult), sg.keys + bk(bu), ffn_act[fb].keys)
            akeys = tuple(k for r in ffn_act for k in r.keys)
            for g in range(NG_WD):
                slot = ffn_wd[g % 2]
                dma("sp", slot.bf(), WD[l][f][g].rearrange("p b k n -> p (b k n)"), WK("WD", l, f, g), slot.keys)
                w = slot.bf(2, FC, 128)
                for b in range(2):
                    db = g * 2 + b
                    bank = nb("d", [4, 5, 6])
                    for k in range(FC):
                        mm(bank, w[:, b, k, :], ffn_act[k].bf(), k == 0, k == FC - 1, slot.keys + ffn_act[k].keys)
                    x_ap, p_ap = xT[db].f32(), banks[bank][:, :]
                    P.add("dve", lambda e, x_ap=x_ap, p_ap=p_ap: e.scalar_tensor_tensor(
                        out=x_ap, in0=p_ap, scalar=0.5, in1=x_ap, op0=ALU.mult, op1=ALU.add),
                        bk(bank) + xT[db].keys, xT[db].keys)

        r_xpre = r_R.sub(RSIZE - KC * T * 4, KC * T * 4)

        def issue_x_input(t):
            src = x_in[t * T:(t + 1) * T, :].rearrange("(s p) d -> p s d", p=128)
            dma("sp", r_xpre.f32(NS, D), src, (), r_xpre.keys)

        def load_x_input(t):
            xtm = r_xpre
            xv = xtm.f32(NS, D)
            for k in range(KC):
                bank = nb("tr", [4, 5, 6])
                for s in range(NS):
                    tr(bank, s * 128, xv[:, s, k * 128:(k + 1) * 128], xtm.keys)
                d_ap, p_ap = xT[k].f32(), banks[bank][:, :]
                if k % 2 == 0:
                    P.add("act", lambda e, d_ap=d_ap, p_ap=p_ap: e.activation(out=d_ap, in_=p_ap, func=AF.Copy),
                          bk(bank), xT[k].keys)
                else:
                    P.add("dve", lambda e, d_ap=d_ap, p_ap=p_ap: e.tensor_copy(out=d_ap, in_=p_ap),
                          bk(bank), xT[k].keys)

        def prefetch_x(l, t):
            dma("sp", r_xpre.f32(), X1[l][t], (("X1", l, t),), r_xpre.keys)

        def load_x_scratch(l, t):
            for k in range(KC):
                s_ap, d_ap = r_xpre.sub(k * T * 4, T * 4).f32(), xT[k].f32()
                kk = r_xpre.sub(k * T * 4, T * 4).keys
                if k % 2 == 0:
                    P.add("act", lambda e, d_ap=d_ap, s_ap=s_ap: e.activation(out=d_ap, in_=s_ap, func=AF.Copy),
                          kk, xT[k].keys)
                else:
                    P.add("dve", lambda e, d_ap=d_ap, s_ap=s_ap: e.tensor_copy(out=d_ap, in_=s_ap), kk, xT[k].keys)

        def store_x_scratch(l, t):
            dma("sp", X1[l][t], r_xT.f32(), r_xT.keys, (("X1", l, t),))

        def load_rope(t):
            rv = r_rope.f32(2, T)
            dma("sp", rv[0:64, 0, :], rope_c_in[:, t * T:(t + 1) * T], (), r_rope.keys)
            dma("sp", rv[0:64, 1, :], rope_s_in[:, t * T:(t + 1) * T], (), r_rope.keys)

        def rope_combine(bx, bs, dst_ap, dst_keys):
            rv = r_rope.f32(2, T)
            t0, t1 = r_tmp[0].f32()[0:64, :], r_tmp[1].f32()[0:64, :]
            xa, xs = banks[bx][0:64, :], banks[bs][0:64, :]
            P.add("dve", lambda e: e.tensor_tensor(out=t0, in0=rv[0:64, 0, :], in1=xa, op=ALU.mult),
                  bk(bx) + r_rope.keys, r_tmp[0].keys)
            P.add("dve", lambda e: e.tensor_tensor(out=t1, in0=rv[0:64, 1, :], in1=xs, op=ALU.mult),
                  bk(bs) + r_rope.keys, r_tmp[1].keys)
            P.add("dve", lambda e: e.tensor_tensor(out=dst_ap, in0=t0, in1=t1, op=ALU.add),
                  r_tmp[0].keys + r_tmp[1].keys, dst_keys)

        def stage_p(l, t):
            norm(xT, KC, D, "mix", l, hT)
            o = V0
            p_ckvT = [r_R.sub(o + k * T * 4, T * 4) for k in range(KVC)]
            o += KVC * T * 4
            p_ckvn = [r_R.sub(o + k * T * 2, T * 2) for k in range(KVC)]
            o += KVC * T * 2
            p_uf = r_R.sub(o, NS * FW * 2)
            o += NS * FW * 2
            p_kt = r_R.sub(o, NH * T * 2)
            o += NH * T * 2
            p_v = r_R.sub(o, NS * NH * 128 * 2)
            o += NS * NH * 128 * 2
            p_kr = r_R.sub(o, T * 2)
            o += T * 2
            assert o <= RSIZE, ("P view", o, RSIZE)
            load_rope(t)
            ufv = p_uf.bf(NS, FW)
            for g in range(FW // 512):
                slot = wslot()
                dma("sp", slot.sub(0, KC * 512 * 2).bf(), WUF[l][g].rearrange("p k n -> p (k n)"),
                    WK("WUF", l, g), slot.keys)
                w = slot.sub(0, KC * 512 * 2).bf(KC, 512)
                for s in range(NS):
                    bank = nb("p", [0, 1, 2, 3])
                    for k in range(KC):
                        mm(bank, hT[k].bf()[:, s * 128:(s + 1) * 128], w[:, k, :], k == 0, k == KC - 1,
                           slot.keys + hT[k].keys)
                    d_ap, p_ap = ufv[:, s, g * 512:(g + 1) * 512], banks[bank][:, :]
                    if s % 2 == 0:
                        P.add("act", lambda e, d_ap=d_ap, p_ap=p_ap: e.activation(out=d_ap, in_=p_ap, func=AF.Copy),
                              bk(bank), p_uf.keys)
                    else:
                        P.add("dve", lambda e, d_ap=d_ap, p_ap=p_ap: e.tensor_copy(out=d_ap, in_=p_ap),
                              bk(bank), p_uf.keys)
            for cb in range(FWC):
                dma("sp", UF[l][cb, :, t * NS:(t + 1) * NS, :], ufv[:, :, cb * 128:(cb + 1) * 128],
                    p_uf.keys, (("UF", l, t, cb),))
            slot = wslot()
            dma("sp", slot.sub(0, KVC * KC * 128 * 2).bf(), WCKV[l].rearrange("p b k n -> p (b k n)"),
                WK("WCKV", l), slot.keys)
            w = slot.sub(0, KVC * KC * 128 * 2).bf(KVC, KC, 128)
            for b in range(KVC):
                bank = nb("p", [0, 1, 2, 3])
                for k in range(KC):
                    mm(bank, w[:, b, k, :], hT[k].bf(), k == 0, k == KC - 1, slot.keys + hT[k].keys)
                d_ap, p_ap = p_ckvT[b].f32(), banks[bank][:, :]
                P.add("act", lambda e, d_ap=d_ap, p_ap=p_ap: e.activation(out=d_ap, in_=p_ap, func=AF.Copy),
                      bk(bank), p_ckvT[b].keys)
            slot = wslot()
            dma("sp", slot.sub(0, 2 * KC * 64 * 2).bf(), WKR[l].rearrange("p a k n -> p (a k n)"),
                WK("WKR", l), slot.keys)
            w = slot.sub(0, 2 * KC * 64 * 2).bf(2, KC, 64)
            bx, bs = nb("p", [0, 1, 2, 3]), nb("p", [0, 1, 2, 3])
            for a, bank in ((0, bx), (1, bs)):
                for k in range(KC):
                    mm(bank, w[:, a, k, :], hT[k].bf(), k == 0, k == KC - 1, slot.keys + hT[k].keys, m=64)
            rope_combine(bx, bs, p_kr.bf()[0:64, :], p_kr.keys)
            dma("sp", KR[l][0:64, t * T:(t + 1) * T], p_kr.bf()[0:64, :], p_kr.keys, (("KR", l, t),))
            norm(p_ckvT, KVC, KVL, "kva", l, p_ckvn)
            ckeys = tuple(k for r in p_ckvn for k in r.keys)
            ktv = p_kt.bf(NH, T)
            vv = p_v.bf(NS, NH * 128)
            for g in range(NH // 4):
                slot = wslot()
                sk = slot.sub(0, 4 * KVC * 128 * 2)
                sv = slot.sub(4 * KVC * 128 * 2, KVC * 512 * 2)
                dma("sp", sk.bf(), WUK[l][g].rearrange("p h k n -> p (h k n)"), WK("WUK", l, g), sk.keys)
                dma("sp", sv.bf(), WUV[l][g].rearrange("p k n -> p (k n)"), WK("WUV", l, g), sv.keys)
                wk, wv = sk.bf(4, KVC, 128), sv.bf(KVC, 512)
                for hh in range(4):
                    h = g * 4 + hh
                    bank = nb("p", [0, 1, 2, 3])
                    for k in range(KVC):
                        mm(bank, wk[:, hh, k, :], p_ckvn[k].bf(), k == 0, k == KVC - 1, sk.keys + p_ckvn[k].keys)
                    d_ap, p_ap = ktv[:, h, :], banks[bank][:, :]
                    P.add("act", lambda e, d_ap=d_ap, p_ap=p_ap: e.activation(out=d_ap, in_=p_ap, func=AF.Copy),
                          bk(bank), p_kt.keys)
                for s in range(NS):
                    bank = nb("p", [0, 1, 2, 3])
                    for k in range(KVC):
                        mm(bank, p_ckvn[k].bf()[:, s * 128:(s + 1) * 128], wv[:, k, :], k == 0, k == KVC - 1,
                           sv.keys + p_ckvn[k].keys)
                    d_ap, p_ap = vv[:, s, g * 512:(g + 1) * 512], banks[bank][:, :]
                    P.add("dve", lambda e, d_ap=d_ap, p_ap=p_ap: e.tensor_copy(out=d_ap, in_=p_ap),
                          bk(bank), p_v.keys)
            dma("sp", KT[l][:, :, t * T:(t + 1) * T].rearrange("h p n -> p h n"), ktv, p_kt.keys, (("KT", l, t),))
            for h in range(NH):
                dma("sp", VS[l][h, :, t * NS:(t + 1) * NS, :], vv[:, :, h * 128:(h + 1) * 128],
                    p_v.keys, (("VS", l, t, h),))

        def stage_m(l, t):
            o = V0
            m_OT = [r_R.sub(o + h * T * 2, T * 2) for h in range(NH)]
            o += NH * T * 2
            m_MT = [r_R.sub(o + k * T * 2, T * 2) for k in range(FWC)]
            o += FWC * T * 2
            PH = o
            all_uf = tuple(("UF", l, tt, cb) for tt in range(NT) for cb in range(FWC))
            all_kt = tuple(("KT", l, tt) for tt in range(NT))
            all_vs = tuple(("VS", l, tt, hh) for tt in range(NT) for hh in range(NH))
            all_kr = tuple(("KR", l, tt) for tt in range(NT)) + WK("KRF", l)
            o = PH
            DSLOT = 4 * 2 * 512 * 2
            NDS = 4
            d_dft = [r_R.sub(o + i * DSLOT, DSLOT) for i in range(NDS)]
            o += NDS * DSLOT
            USLOT = NKC * 128 * 2
            d_uf = [r_R.sub(o + i * USLOT, USLOT) for i in range(4)]
            o += 4 * USLOT
            if 2 * FWC <= NH:
                d_A = [r_R.sub(V0 + k * T * 2, T * 2) for k in range(FWC)]
                d_B = [r_R.sub(V0 + (FWC + k) * T * 2, T * 2) for k in range(FWC)]
            else:
                d_A = [r_R.sub(o + k * T * 2, T * 2) for k in range(FWC)]
                o += FWC * T * 2
                d_B = [r_R.sub(o + k * T * 2, T * 2) for k in range(FWC)]
                o += FWC * T * 2
            assert o <= RSIZE, ("DFT view", o, RSIZE)
            di = 0
            for ps in range(FWC // 4):
                uvs = []
                for ci in range(4):
                    cb = ps * 4 + ci
                    us = d_uf[ci]
                    dma("sp", us.bf(), UF[l][cb].rearrange("p j n -> p (j n)"), all_uf, us.keys)
                    uvs.append((us, us.bf(NKC, 128)))
                for ng in range(NKC // 4):
                    ds = d_dft[di % NDS]
                    di += 1
                    dma("sp", ds.bf(), DFT[t, ng].rearrange("p j c n -> p (j c n)"), WK("DFT", t, ng), ds.keys)
                    dv = ds.bf(4, 2, 512)
                    for j in range(4):
                        kc = ng * 4 + j
                        for ci in range(4):
                            us, uv = uvs[ci]
                            mm(ci, uv[:, kc, :], dv[:, j, 0, :], kc == 0, kc == NKC - 1, us.keys + ds.keys)
                            mm(4 + ci, uv[:, kc, :], dv[:, j, 1, :], kc == 0, kc == NKC - 1, us.keys + ds.keys)
                for ci in range(4):
                    cb = ps * 4 + ci
                    a_ap, b_ap = d_A[cb].bf(), d_B[cb].bf()
                    pa, pb = banks[ci][:, :], banks[4 + ci][:, :]
                    P.add("act", lambda e, a_ap=a_ap, pa=pa: e.activation(out=a_ap, in_=pa, func=AF.Copy),
                          bk(ci), d_A[cb].keys)
                    P.add("dve", lambda e, b_ap=b_ap, pb=pb: e.tensor_copy(out=b_ap, in_=pb),
                          bk(4 + ci), d_B[cb].keys)
            cd = r_cdft.bf(2, 2, GD)
            for grp in range(FW // GD):
                for jb in range(2):
                    bank = nb("p", [4, 5, 6])
                    n_acc = 0
                    for cs, src in ((0, d_A), (1, d_B)):
                        for kk in range(2):
                            mm(bank, cd[:, cs, kk, jb * 128:(jb + 1) * 128], src[grp * 2 + kk].bf(),
                               n_acc == 0, n_acc == 3, r_cdft.keys + src[grp * 2 + kk].keys)
                            n_acc += 1
                    d_ap, p_ap = m_MT[grp * 2 + jb].bf(), banks[bank][:, :]
                    P.add("dve", lambda e, d_ap=d_ap, p_ap=p_ap: e.tensor_copy(out=d_ap, in_=p_ap),
                          bk(bank), m_MT[grp * 2 + jb].keys)
            norm(xT, KC, D, "mix", l, hT)
            load_rope(t)
            o = PH
            a_cqT = [r_R.sub(o + k * T * 4, T * 4) for k in range(QC)]
            o += QC * T * 4
            a_cqn = [r_R.sub(o + k * T * 2, T * 2) for k in range(QC)]
            o += QC * T * 2
            a_qn = [r_R.sub(o + i * T * 2, T * 2) for i in range(2)]
            o += 2 * T * 2
            a_qr = [r_R.sub(o + i * T * 2, T * 2) for i in range(2)]
            o += 2 * T * 2
            a_kr = r_R.sub(o, NTOK * 2)
            o += NTOK * 2
            a_kt = [r_R.sub(o + i * NTOK * 2, NTOK * 2) for i in range(2)]
            o += 2 * NTOK * 2
            a_v = [r_R.sub(o + i * NTOK * 2, NTOK * 2) for i in range(2)]
            o += 2 * NTOK * 2
            NPT = 4
            a_pt = [r_R.sub(o + i * T * 2, T * 2) for i in range(NPT)]
            o += NPT * T * 2
            a_rinv = r_R.sub(o, T * 4)
            o += T * 4
            a_acc = [r_R.sub(o + i * T * 4, T * 4) for i in range(2)]
            o += 2 * T * 4
            assert o <= RSIZE, ("attention view", o, RSIZE)
            slot = wslot()
            dma("sp", slot.sub(0, QC * KC * 128 * 2).bf(), WCQ[l].rearrange("p b k n -> p (b k n)"),
                WK("WCQ", l), slot.keys)
            w = slot.sub(0, QC * KC * 128 * 2).bf(QC, KC, 128)
            for b in range(QC):
                bank = nb("p", [4, 5, 6])
                for k in range(KC):
                    mm(bank, w[:, b, k, :], hT[k].bf(), k == 0, k == KC - 1, slot.keys + hT[k].keys)
                d_ap, p_ap = a_cqT[b].f32(), banks[bank][:, :]
                P.add("act", lambda e, d_ap=d_ap, p_ap=p_ap: e.activation(out=d_ap, in_=p_ap, func=AF.Copy),
                      bk(bank), a_cqT[b].keys)
            norm(a_cqT, QC, QL, "qa", l, a_cqn)
            krv = a_kr.bf()
            dma("sp", krv[0:66, :], KR[l][:, :], all_kr, a_kr.keys)
            for i in range(2):
                dma("sp", a_qr[i].bf()[64:66, :], QF[:, t * T:(t + 1) * T], WK("QF",), a_qr[i].keys)
            scale = float((128 + 64) ** -0.5)
            NQ = QC * 128
            rv = r_rope.f32(2, T)
            QB = 7
            wq_cur = {}

            def kv_load(h):
                dma("sp", a_kt[h % 2].bf(), KT[l][h], all_kt, a_kt[h % 2].keys)
                dma("sp", a_v[h % 2].bf(), VS[l][h].rearrange("p j n -> p (j n)"), all_vs, a_v[h % 2].keys)

            def wq_load(g):
                qslot = wslot()
                qs = qslot.sub(0, 4 * (NQ + 2 * QC * 64) * 2)
                dma("sp", qs.bf(), WUQ[l][g].rearrange("p h n -> p (h n)"), WK("WUQ", l, g), qs.keys)
                wq_cur[g] = (qs, qs.bf(4, NQ + 2 * QC * 64))

            def q_part(h, part):
                g, hh = divmod(h, 4)
                if g not in wq_cur:
                    wq_load(g)
                qs, wq = wq_cur[g]
                qn, qr = a_qn[h % 2], a_qr[h % 2]
                t0, t1 = r_tmp[0].f32()[0:64, :], r_tmp[1].f32()[0:64, :]
                if part == 0:
                    for k in range(QC):
                        mm(QB, wq[:, hh, k * 128:(k + 1) * 128], a_cqn[k].bf(), k == 0, k == QC - 1,
                           qs.keys + a_cqn[k].keys)
                    d_ap, p_ap = qn.bf(), banks[QB][:, :]
                    P.add("dve", lambda e, d_ap=d_ap, p_ap=p_ap: e.tensor_copy(out=d_ap, in_=p_ap), bk(QB), qn.keys)
                elif part == 1:
                    for k in range(QC):
                        mm(QB, wq[:, hh, NQ + k * 64:NQ + (k + 1) * 64], a_cqn[k].bf(), k == 0, k == QC - 1,
                           qs.keys + a_cqn[k].keys, m=64)
                    xa = banks[QB][0:64, :]
                    P.add("dve", lambda e: e.tensor_tensor(out=t0, in0=rv[0:64, 0, :], in1=xa, op=ALU.mult),
                          bk(QB) + r_rope.keys, r_tmp[0].keys)
                else:
                    for k in range(QC):
                        mm(QB, wq[:, hh, NQ + QC * 64 + k * 64:NQ + QC * 64 + (k + 1) * 64], a_cqn[k].bf(),
                           k == 0, k == QC - 1, qs.keys + a_cqn[k].keys, m=64)
                    xs = banks[QB][0:64, :]
                    d_ap = qr.bf()[0:64, :]
                    P.add("dve", lambda e: e.tensor_tensor(out=t1, in0=rv[0:64, 1, :], in1=xs, op=ALU.mult),
                          bk(QB) + r_rope.keys, r_tmp[1].keys)
                    P.add("dve", lambda e: e.tensor_tensor(out=d_ap, in0=t0, in1=t1, op=ALU.add),
                          r_tmp[0].keys + r_tmp[1].keys, qr.keys)

            items = [(h, kc) for h in range(NH) for kc in range(NKC)]
            NI = len(items)
            LOOK = 2

            def emit_st(i):
                h, kc = items[i]
                kts, qn, qr = a_kt[h % 2], a_qn[h % 2], a_qr[h % 2]
                bst = i % 3
                mm(bst, kts.bf()[:, kc * 128:(kc + 1) * 128], qn.bf(), True, False, kts.keys + qn.keys)
                mm(bst, krv[0:66, kc * 128:(kc + 1) * 128], qr.bf()[0:66, :], False, True, a_kr.keys + qr.keys)

            kv_load(0)
            if NH > 1:
                kv_load(1)
            for part in range(3):
                q_part(0, part)
            for i in range(min(LOOK, NI)):
                emit_st(i)
            qpts = {1: 0, NKC // 4 + 1: 1, NKC // 2 + 1: 2}
            assert NKC // 2 + 1 < NKC - LOOK
            for i in range(NI):
                h, kc = items[i]
                if i + LOOK < NI:
                    emit_st(i + LOOK)
                bst, bo, br = i % 3, 3 + h % 2, 5 + h % 2
                vs = a_v[h % 2]
                vsv = vs.bf(NKC, 128)
                pt = a_pt[i % NPT]
                pt_ap, st_ap = pt.bf(), banks[bst][:, :]
                P.add("act", lambda e, pt_ap=pt_ap, st_ap=st_ap: e.activation(
                    out=pt_ap, in_=st_ap, func=AF.Exp, scale=scale), bk(bst), pt.keys)
                mm(bo, vsv[:, kc, :], pt_ap, kc == 0, kc == NKC - 1, vs.keys + pt.keys)
                acc = a_acc[h % 2]
                acc_ap = acc.f32()
                if kc % 3 == 0:
                    mm(br, ones_bf, pt_ap, kc == 0, False, r_ones.keys + pt.keys)
                elif kc == 1:
                    P.add("dve", lambda e, acc_ap=acc_ap, pt_ap=pt_ap: e.tensor_copy(out=acc_ap, in_=pt_ap),
                          pt.keys, acc.keys)
                else:
                    P.add("dve", lambda e, acc_ap=acc_ap, pt_ap=pt_ap: e.tensor_tensor(
                        out=acc_ap, in0=acc_ap, in1=pt_ap, op=ALU.add), pt.keys + acc.keys, acc.keys)
                if kc == NKC - 1:
                    mm(br, ones32, acc_ap, False, True, r_ones32.keys + acc.keys)
                if kc in qpts and h + 1 < NH:
                    q_part(h + 1, qpts[kc])
                if kc == NKC - 1:
                    rinv = a_rinv.f32()
                    pr, po = banks[br][:, :], banks[bo][:, :]
                    o_ap = m_OT[h].bf()
                    P.add("dve", lambda e, pr=pr, rinv=rinv: e.reciprocal(out=rinv, in_=pr), bk(br), a_rinv.keys)
                    P.add("dve", lambda e, po=po, o_ap=o_ap, rinv=rinv: e.tensor_tensor(
                        out=o_ap, in0=rinv, in1=po, op=ALU.mult), bk(bo) + a_rinv.keys, m_OT[h].keys)
                    if h + 2 < NH:
                        kv_load(h + 2)
                    if h % 4 == 1 and h // 4 + 1 < NH // 4:
                        wq_load(h // 4 + 1)
            o = PH
            g_mT = [r_R.sub(o + k * T * 2, T * 2) for k in range(KC)]
            o += KC * T * 2
            assert o <= RSIZE
            for gI in range(D // 256):
                if gI % 2 == 0:
                    so, sm, sf = wslot(), wslot(), None
                    s_mo = so.sub(0, 4 * NH * 128 * 2)
                    dma("sp", s_mo.bf(), WMO[l][gI // 2].rearrange("p b k n -> p (b k n)"),
                        WK("WMO", l, gI // 2), s_mo.keys)
                    wmo = s_mo.bf(4, NH, 128)
                    s_fo = sm.sub(0, 4 * FWC * 128 * 2)
                    dma("sp", s_fo.bf(), WFO[l][gI // 2].rearrange("p b k n -> p (b k n)"),
                        WK("WFO", l, gI // 2), s_fo.keys)
                    wfo = s_fo.bf(4, FWC, 128)
                gslot = r_R.sub(RSIZE - 2 * 2 * KC * 128 * 2 * (1 + gI % 2), 2 * 2 * KC * 128 * 2)
                assert RSIZE - 2 * (2 * 2 * KC * 128 * 2) >= o
                dma("sp", gslot.bf(), WGATE[l][gI].rearrange("p a b k n -> p (a b k n)"), WK("WGATE", l, gI), gslot.keys)
                wg = gslot.bf(2, 2, KC, 128)
                for b in range(2):
                    db = gI * 2 + b
                    b4 = db % 4
                    bya, byb, bga, bgb = nb("ya", [0, 1]), nb("yb", [2, 3]), nb("ga", [4, 5]), nb("gb", [6, 7])
                    for k in range(FWC):
                        mm(bya, wfo[:, b4, k, :], m_MT[k].bf(), k == 0, k == FWC - 1, s_fo.keys + m_MT[k].keys)
                    for k in range(NH):
                        mm(byb, wmo[:, b4, k, :], m_OT[k].bf(), k == 0, k == NH - 1, s_mo.keys + m_OT[k].keys)
                    for ab, bank in ((0, bga), (1, bgb)):
                        for k in range(KC):
                            mm(bank, wg[:, ab, b, k, :], hT[k].bf(), k == 0, k == KC - 1, gslot.keys + hT[k].keys)
                    sa, sb_ = r_sg[0], r_sg[1]
                    sa_ap, sb_ap = sa.f32(), sb_.f32()
                    pga, pgb, pya, pyb = banks[bga][:, :], banks[bgb][:, :], banks[bya][:, :], banks[byb][:, :]
                    ba_ap, bb_ap = vcol("bg", l, db), vcol("bg", l, KC + db)
                    P.add("act", lambda e, sa_ap=sa_ap, pga=pga, ba_ap=ba_ap: e.activation(
                        out=sa_ap, in_=pga, func=AF.Sigmoid, bias=ba_ap), bk(bga) + r_vecs.keys, sa.keys)
                    P.add("act", lambda e, sb_ap=sb_ap, pgb=pgb, bb_ap=bb_ap: e.activation(
                        out=sb_ap, in_=pgb, func=AF.Sigmoid, bias=bb_ap), bk(bgb) + r_vecs.keys, sb_.keys)
                    t0, t1 = r_tmp[0].f32(), r_tmp[1].f32()
                    m_ap = g_mT[db].bf()
                    P.add("dve", lambda e, sa_ap=sa_ap, pya=pya, t0=t0: e.tensor_tensor(
                        out=t0, in0=sa_ap, in1=pya, op=ALU.mult), sa.keys + bk(bya), r_tmp[0].keys)
                    P.add("dve", lambda e, sb_ap=sb_ap, pyb=pyb, t1=t1: e.tensor_tensor(
                        out=t1, in0=sb_ap, in1=pyb, op=ALU.mult), sb_.keys + bk(byb), r_tmp[1].keys)
                    P.add("dve", lambda e, t0=t0, t1=t1, m_ap=m_ap: e.tensor_tensor(
                        out=m_ap, in0=t0, in1=t1, op=ALU.add), r_tmp[0].keys + r_tmp[1].keys, g_mT[db].keys)
            for g in range(D // 512):
                slot = wslot()
                s_o = slot.sub(0, 4 * KC * 128 * 2)
                dma("sp", s_o.bf(), WOUT[l][g].rearrange("p b k n -> p (b k n)"), WK("WOUT", l, g), s_o.keys)
                w = s_o.bf(4, KC, 128)
                for b in range(4):
                    db = g * 4 + b
                    bank = nb("o", [4, 5, 6])
                    for k in range(KC):
                        mm(bank, w[:, b, k, :], g_mT[k].bf(), k == 0, k == KC - 1, s_o.keys + g_mT[k].keys)
                    x_ap, p_ap = xT[db].f32(), banks[bank][:, :]
                    P.add("dve", lambda e, x_ap=x_ap, p_ap=p_ap: e.tensor_tensor(
                        out=x_ap, in0=x_ap, in1=p_ap, op=ALU.add), bk(bank) + xT[db].keys, xT[db].keys)

        def final_out(t):
            yT = [r_R.sub(k * T * 4, T * 4) for k in range(KC)]
            ytm = r_R.sub(KC * T * 4, NS * D * 4)
            assert KC * T * 4 + NS * D * 4 <= RSIZE
            norm(xT, KC, D, "final", 0, yT, dst_f32=True)
            yv = ytm.f32(NS, D)
            for s in range(NS):
                for k4 in range(KC // 4):
                    bank = nb("tr", [4, 5, 6])
                    for j in range(4):
                        k = k4 * 4 + j
                        tr(bank, j * 128, yT[k].f32()[:, s * 128:(s + 1) * 128], yT[k].keys)
                    d_ap, p_ap = yv[:, s, k4 * 512:(k4 + 1) * 512], banks[bank][:, :]
                    if k4 % 2 == 0:
                        P.add("act", lambda e, d_ap=d_ap, p_ap=p_ap: e.activation(out=d_ap, in_=p_ap, func=AF.Copy),
                              bk(bank), ytm.keys)
                    else:
                        P.add("dve", lambda e, d_ap=d_ap, p_ap=p_ap: e.tensor_copy(out=d_ap, in_=p_ap),
                              bk(bank), ytm.keys)
            dst = y_out[t * T:(t + 1) * T, :].rearrange("(s p) d -> p s d", p=128)
            dma("sp", dst, yv, ytm.keys, (("Y", t),))

        run_jobs(jobs_dft()[-(3 + L):])
        dma("sp", r_cdft.bf(), CDFT.rearrange("p a k n -> p (a k n)"), WK("CDFT",), r_cdft.keys)
        run_jobs(jobs_ffn(0, 0))
        run_jobs(jobs_projA(0))
        def needed_by_phase(l):
            j = jobs_mix(l) + jobs_ffn(l, 1)
            if l + 1 < L:
                j = j + jobs_ffn(l + 1, 0) + jobs_projA(l + 1)
            return j

        later0 = spread(jobs_dft()[:-(3 + L)] + needed_by_phase(0), NT)
        issue_x_input(0)
        for t in range(NT):
            load_x_input(t)
            ffn(0, 0)
            if t + 1 < NT:
                issue_x_input(t + 1)
            else:
                prefetch_x(0, 0)
            stage_p(0, t)
            store_x_scratch(0, t)
            run_jobs(later0[t])
        for l in range(L):
            last = (l == L - 1)
            if not last:
                later = spread(needed_by_phase(l + 1), NT)
            for t in range(NT):
                load_x_scratch(l, t)
                stage_m(l, t)
                ffn(l, 1)
                if last:
                    if t + 1 < NT:
                        prefetch_x(l, t + 1)
                    final_out(t)
                else:
                    ffn(l + 1, 0)
                    if t + 1 < NT:
                        prefetch_x(l, t + 1)
                    else:
                        prefetch_x(l + 1, 0)
                    stage_p(l + 1, t)
                    store_x_scratch(l + 1, t)
                    run_jobs(later[t])

        P.emit(nc, block, sems, dsems)
    return nc


def _tables(cfg, seq_len):
    c = cfg
    nseq = c.NTOK // seq_len
    half = 32
    inv_freq = (1.0 / (ROPE_THETA ** (np.arange(half, dtype=np.float32) / half))).astype(np.float32)
    pos = (np.arange(c.NTOK) % seq_len).astype(np.float32)
    ang = (pos[None, :] * inv_freq[:, None]).astype(np.float32)
    cos, sin = np.cos(ang).astype(np.float32), np.sin(ang).astype(np.float32)
    rope_c = np.concatenate([cos, cos], 0)
    rope_s = np.concatenate([-sin, sin], 0)
    seq_id = np.arange(c.NTOK) // seq_len
    flags = np.zeros((4, c.NTOK), np.float32)
    a = (seq_id % 2 == 0).astype(np.float32)
    b = 1.0 - a
    if nseq == 1:
        b[:] = 0.0
    flags[0], flags[1] = -b, -a
    flags[2], flags[3] = BIGMASK * a, BIGMASK * b
    assert nseq <= 2
    n = np.arange(seq_len, dtype=np.float64)
    kn = np.outer(n, n) % seq_len
    angd = 2.0 * np.pi * kn / seq_len
    cs = (np.cos(angd) / np.sqrt(seq_len)).astype(np.float32)
    ss = (np.sin(angd) / np.sqrt(seq_len)).astype(np.float32)
    dft_c = np.zeros((c.NTOK, c.NTOK), np.float32)
    dft_s = np.zeros((c.NTOK, c.NTOK), np.float32)
    for i in range(nseq):
        sl = slice(i * seq_len, (i + 1) * seq_len)
        dft_c[sl, sl] = cs
        dft_s[sl, sl] = ss
    j = np.arange(c.GD, dtype=np.float64)
    angc = 2.0 * np.pi * (np.outer(j, j) % c.GD) / c.GD
    cdft = np.stack([np.cos(angc) / np.sqrt(c.GD), -np.sin(angc) / np.sqrt(c.GD)]).astype(np.float32)
    return dict(rope_c=rope_c, rope_s=rope_s, flags=flags, dft_c=dft_c, dft_s=dft_s, cdft=cdft)


def _vecs(cfg, inp):
    c = cfg
    v = np.zeros((128, c.NVEC), np.float32)

    def put(nm, l, arr):
        o = c.vec_off[(nm, l)]
        n = arr.shape[0] // 128
        v[:, o:o + n] = arr.reshape(n, 128).T
    for l in range(c.L):
        put("ffn1", l, inp["ffn1_norm"][l])
        put("mix", l, inp["mix_norm"][l])
        put("ffn2", l, inp["ffn2_norm"][l])
        put("qa", l, inp["q_a_norm"][l])
        put("kva", l, inp["kv_a_norm"][l])
        put("bg", l, inp["b_gate"][l])
    put("final", 0, inp["final_norm"])
    return v


_WNAMES = ("ffn1_w_gate", "ffn1_w_up", "ffn1_w_down", "ffn2_w_gate", "ffn2_w_up", "ffn2_w_down",
           "w_in", "w_uq", "w_ukv", "w_fourier", "w_mla_o", "w_out")


def run_cores(cfg, inp, xs, seq_lens):
    nc = build(cfg)
    shared = {k: np.ascontiguousarray(np.asarray(inp[k], dtype=np.float32)) for k in _WNAMES}
    shared["vecs"] = _vecs(cfg, {k: np.asarray(v) for k, v in inp.items()})
    shared["ident32"] = np.eye(128, dtype=np.float32)
    tabs = {}
    in_maps = []
    for x, sl in zip(xs, seq_lens):
        if sl not in tabs:
            tabs[sl] = _tables(cfg, sl)
        m = dict(shared)
        m.update(tabs[sl])
        m["x"] = np.ascontiguousarray(x, dtype=np.float32)
        in_maps.append(m)
    res = run_bass_kernel_spmd(nc, in_maps, core_ids=list(range(len(xs))))
    return [np.asarray(r["y"], dtype=np.float32) for r in res.results]


def kernel(**inputs):
    cfg = Cfg()
    xp = np.asarray(inputs["x_prompt"], dtype=np.float32)
    xsm = np.asarray(inputs["x_sample"], dtype=np.float32)
    B, S, D = xp.shape
    DB, DS, _ = xsm.shape
    xs = [xp[i] for i in range(B)] + [xsm[2 * j:2 * j + 2].reshape(2 * DS, D) for j in range(DB // 2)]
    seq_lens = [S] * B + [DS] * (DB // 2)
    outs = run_cores(cfg, inputs, xs, seq_lens)
    y_prompt = np.stack(outs[:B], 0)
    y_sample = np.concatenate([o.reshape(2, DS, D) for o in outs[B:]], 0)
    return (y_prompt, y_sample)
```

```python
import numpy as np
from contextlib import ExitStack
import concourse.bass as bass
import concourse.mybir as mybir
from concourse.bass_utils import run_bass_kernel_spmd

F32 = mybir.dt.float32
BF16 = mybir.dt.bfloat16
AF = mybir.ActivationFunctionType
ALU = mybir.AluOpType

NORM_EPS = 1e-6
ROPE_THETA = 10000.0
BIGMASK = 4096.0


class Cfg:
    def __init__(s, D=2048, DFF=5632, NH=16, QL=512, KVL=512, FW=1024, GD=256, NTOK=4096, L=2, T=512):
        s.D, s.DFF, s.NH, s.QL, s.KVL, s.FW, s.GD, s.NTOK, s.L, s.T = D, DFF, NH, QL, KVL, FW, GD, NTOK, L, T
        s.KC = D // 128
        s.FC = DFF // 128
        s.NT = NTOK // T
        s.NS = T // 128
        s.QC = QL // 128
        s.KVC = KVL // 128
        s.FWC = FW // 128
        s.HC = NH
        s.NKC = NTOK // 128
        s.DIN = FW + QL + KVL + 64 + 2 * D
        s.c_cq = FW
        s.c_ckv = FW + QL
        s.c_kr = FW + QL + KVL
        s.c_g = s.c_kr + 64
        s.vec_off = {}
        o = 0
        for l in range(L):
            for nm, n in (("ffn1", s.KC), ("mix", s.KC), ("ffn2", s.KC), ("qa", s.QC), ("kva", s.KVC),
                          ("bg", 2 * s.KC)):
                s.vec_off[(nm, l)] = o
                o += n
        s.vec_off[("final", 0)] = o
        o += s.KC
        s.NVEC = o


class Op:
    __slots__ = ("eng", "fn", "deps", "dma", "dsem", "dval", "sig", "sigval")


class Prog:
    ENGS = ("sp", "pool", "pe", "act", "dve")

    def __init__(self, n_sp=40, n_pool=24):
        self.ops = []
        self.lw = {}
        self.rd = {}
        self.dma_n = {"sp": n_sp, "pool": n_pool}
        self.dma_rr = {"sp": 0, "pool": 0}
        self.dma_last = {"sp": [None] * n_sp, "pool": [None] * n_pool}
        self.dma_cnt = {"sp": [0] * n_sp, "pool": [0] * n_pool}

    def add(self, eng, fn, reads=(), writes=(), dma=False):
        deps = set()
        lw, rd = self.lw, self.rd
        for k in reads:
            j = lw.get(k)
            if j is not None:
                deps.add(j)
        for k in writes:
            j = lw.get(k)
            if j is not None:
                deps.add(j)
            r = rd.get(k)
            if r:
                deps.update(r)
        idx = len(self.ops)
        op = Op()
        op.eng, op.fn, op.dma, op.sig, op.sigval = eng, fn, dma, False, 0
        op.dsem = op.dval = None
        if dma:
            s = self.dma_rr[eng]
            self.dma_rr[eng] = (s + 1) % self.dma_n[eng]
            prev = self.dma_last[eng][s]
            if prev is not None:
                deps.add(prev)
            self.dma_last[eng][s] = idx
            self.dma_cnt[eng][s] += 1
            op.dsem = (eng, s)
            op.dval = 16 * self.dma_cnt[eng][s]
        ops = self.ops
        if eng == "pe":
            deps = {j for j in deps if ops[j].dma or ops[j].eng != "pe"}
        op.deps = deps
        for j in deps:
            if not ops[j].dma:
                ops[j].sig = True
        for k in reads:
            rd.setdefault(k, []).append(idx)
        for k in writes:
            lw[k] = idx
            rd[k] = []
        ops.append(op)
        return idx

    def emit(self, nc, block, sems, dsems):
        ops = self.ops
        cnt = {e: 0 for e in self.ENGS}
        for op in ops:
            if op.sig:
                cnt[op.eng] += 1
                op.sigval = cnt[op.eng]
        by_eng = {e: [] for e in self.ENGS}
        for op in ops:
            by_eng[op.eng].append(op)

        def run(e, name):
            waited = {}
            for op in by_eng[name]:
                need = {}
                for j in op.deps:
                    d = ops[j]
                    if d.dma:
                        key, val = ("d",) + d.dsem, d.dval
                    else:
                        key, val = ("c", d.eng), d.sigval
                    if val > need.get(key, 0):
                        need[key] = val
                for key, val in need.items():
                    if waited.get(key, 0) >= val:
                        continue
                    waited[key] = val
                    sem = dsems[key[1]][key[2]] if key[0] == "d" else sems[key[1]]
                    e.wait_ge(sem, val)
                ins = op.fn(e)
                if op.dma:
                    ins.then_inc(dsems[op.dsem[0]][op.dsem[1]], 16)
                elif op.sig:
                    ins.then_inc(sems[name], 1)
            if name in ("sp", "pool"):
                for s in range(self.dma_n[name]):
                    c = self.dma_cnt[name][s]
                    if c and waited.get(("d", name, s), 0) < 16 * c:
                        e.wait_ge(dsems[name][s], 16 * c)

        @block.sync
        def _(e):
            run(e, "sp")

        @block.gpsimd
        def _(e):
            run(e, "pool")

        @block.tensor
        def _(e):
            run(e, "pe")

        @block.scalar
        def _(e):
            run(e, "act")

        @block.vector
        def _(e):
            run(e, "dve")


class Reg:
    __slots__ = ("arena", "off", "n", "keys")

    def __init__(self, arena, off, n):
        assert off % 4 == 0 and n % 4 == 0
        self.arena, self.off, self.n = arena, off, n
        self.keys = tuple(("sb", b) for b in range(off // 512, (off + n - 1) // 512 + 1))

    def sub(self, off, n):
        assert off + n <= self.n, (off, n, self.n)
        return Reg(self.arena, self.off + off, n)

    def f32(self, *shape):
        ap = self.arena[:, self.off // 4:(self.off + self.n) // 4]
        return _shape(ap, shape)

    def bf(self, *shape):
        ap = self.arena[:, self.off // 4:(self.off + self.n) // 4].bitcast(BF16)
        return _shape(ap, shape)


def _shape(ap, shape):
    if len(shape) <= 1:
        return ap
    if len(shape) == 2:
        return ap.rearrange("p (a b) -> p a b", a=shape[0])
    if len(shape) == 3:
        return ap.rearrange("p (a b c) -> p a b c", a=shape[0], b=shape[1])
    if len(shape) == 4:
        return ap.rearrange("p (a b c d) -> p a b c d", a=shape[0], b=shape[1], c=shape[2])
    raise ValueError(shape)


class Alloc:
    def __init__(self, arena, base, limit):
        self.arena, self.o, self.limit = arena, base, limit

    def get(self, n):
        r = Reg(self.arena, self.o, n)
        self.o += (n + 511) // 512 * 512
        assert self.o <= self.limit, ("SBUF overflow", self.o, self.limit)
        return r


def build(cfg):
    c = cfg
    D, DFF, NH, QL, KVL, FW, GD, NTOK, L, T = c.D, c.DFF, c.NH, c.QL, c.KVL, c.FW, c.GD, c.NTOK, c.L, c.T
    KC, FC, NT, NS, QC, KVC, FWC, NKC = c.KC, c.FC, c.NT, c.NS, c.QC, c.KVC, c.FWC, c.NKC
    assert T == 512 and DFF % 256 == 0 and D % 512 == 0 and NH % 4 == 0 and FW % 512 == 0 and GD == 256
    nc = bass.Bass("TRN2", target_bir_lowering=False)
    P = Prog()

    def din(name, shape, dt=F32):
        return nc.dram_tensor(name, list(shape), dt, kind="ExternalInput").ap()

    def dscr(name, shape, dt=BF16):
        return nc.dram_tensor(name, list(shape), dt, kind="Internal").ap()

    x_in = din("x", [NTOK, D])
    w_gate = [din("ffn1_w_gate", [L, D, DFF]), din("ffn2_w_gate", [L, D, DFF])]
    w_up = [din("ffn1_w_up", [L, D, DFF]), din("ffn2_w_up", [L, D, DFF])]
    w_down = [din("ffn1_w_down", [L, DFF, D]), din("ffn2_w_down", [L, DFF, D])]
    w_in = din("w_in", [L, D, c.DIN])
    w_uq = din("w_uq", [L, QL, NH * 192])
    w_ukv = din("w_ukv", [L, KVL, NH * 256])
    w_fourier = din("w_fourier", [L, FW, D])
    w_mla_o = din("w_mla_o", [L, NH * 128, D])
    w_out = din("w_out", [L, D, D])
    vecs_in = din("vecs", [128, c.NVEC])
    gfin_in = din("ident32", [128, 128])
    rope_c_in = din("rope_c", [64, NTOK])
    rope_s_in = din("rope_s", [64, NTOK])
    flags_in = din("flags", [4, NTOK])
    dft_c_in = din("dft_c", [NTOK, NTOK])
    dft_s_in = din("dft_s", [NTOK, NTOK])
    cdft_in = din("cdft", [2, GD, GD])
    y_out = nc.dram_tensor("y", [NTOK, D], F32, kind="ExternalOutput").ap()

    NG_FF = DFF // 256
    NG_WD = D // 256
    WGU = [[dscr(f"WGU{l}_{f}", [NG_FF, 128, 2, 2, KC, 128]) for f in range(2)] for l in range(L)]
    WD = [[dscr(f"WD{l}_{f}", [NG_WD, 128, 2, FC, 128]) for f in range(2)] for l in range(L)]
    WUF = [dscr(f"WUF{l}", [FW // 512, 128, KC, 512]) for l in range(L)]
    WCQ = [dscr(f"WCQ{l}", [128, QC, KC, 128]) for l in range(L)]
    WCKV = [dscr(f"WCKV{l}", [128, KVC, KC, 128]) for l in range(L)]
    WKR = [dscr(f"WKR{l}", [128, 2, KC, 64]) for l in range(L)]
    WGATE = [dscr(f"WGATE{l}", [D // 256, 128, 2, 2, KC, 128]) for l in range(L)]
    WOUT = [dscr(f"WOUT{l}", [D // 512, 128, 4, KC, 128]) for l in range(L)]
    WMO = [dscr(f"WMO{l}", [D // 512, 128, 4, NH, 128]) for l in range(L)]
    WFO = [dscr(f"WFO{l}", [D // 512, 128, 4, FWC, 128]) for l in range(L)]
    WUQ = [dscr(f"WUQ{l}", [NH // 4, 128, 4, QC * 128 + 2 * QC * 64]) for l in range(L)]
    WUK = [dscr(f"WUK{l}", [NH // 4, 128, 4, KVC, 128]) for l in range(L)]
    WUV = [dscr(f"WUV{l}", [NH // 4, 128, KVC, 512]) for l in range(L)]
    DFT = dscr("DFT", [NT, NKC // 4, 128, 4, 2, 512])
    CDFT = dscr("CDFT", [128, 2, 2, GD])
    QF = dscr("QF", [2, NTOK])
    X1 = [dscr(f"X1_{l}", [NT, 128, KC * T], F32) for l in range(L)]
    UF = [dscr(f"UF{l}", [FWC, 128, NKC, 128]) for l in range(L)]
    KT = [dscr(f"KT{l}", [NH, 128, NTOK]) for l in range(L)]
    VS = [dscr(f"VS{l}", [NH, 128, NKC, 128]) for l in range(L)]
    KR = [dscr(f"KR{l}", [66, NTOK]) for l in range(L)]

    es = ExitStack()
    with es:
        ARENA_BYTES = 200 * 1024
        arena = es.enter_context(nc.sbuf_tensor("arena", [128, ARENA_BYTES // 4], F32))
        banks = [es.enter_context(nc.psum_tensor(f"bank{i}", [128, 512], F32)) for i in range(8)]
        sems = {e: es.enter_context(nc.semaphore(f"s_{e}")) for e in ("pool", "pe", "act", "dve")}
        dsems = {q: [es.enter_context(nc.semaphore(f"d_{q}{i}")) for i in range(P.dma_n[q])]
                 for q in ("sp", "pool")}
        block = es.enter_context(nc.Block())

        def bk(i):
            return (("ps", i),)

        A = Alloc(arena, 0, ARENA_BYTES)
        r_xT = A.get(KC * T * 4)
        r_hT = A.get(KC * T * 2)
        r_vecs = A.get(c.NVEC * 4)
        r_id32 = A.get(128 * 4)
        r_ones = A.get(128 * 2)
        r_ones32 = A.get(128 * 4)
        r_eps = A.get(512)
        r_cdft = A.get(2 * 2 * GD * 2)
        r_sq = [A.get(T * 2) for _ in range(4)]
        r_std = A.get(T * 4)
        r_rstd = A.get(T * 4)
        r_sg = [A.get(T * 4) for _ in range(2)]
        r_rope = A.get(2 * T * 4)
        r_tmp = [A.get(T * 4) for _ in range(2)]
        R0 = A.o
        RSIZE = ARENA_BYTES - R0
        r_R = Reg(arena, R0, RSIZE)

        xT = [r_xT.sub(k * T * 4, T * 4) for k in range(KC)]
        hT = [r_hT.sub(k * T * 2, T * 2) for k in range(KC)]
        vecs = r_vecs.f32()
        id32 = r_id32.f32()
        ones_bf = r_ones.bf()
        eps_ap = r_eps.f32()[:, 0:1]

        def vcol(nm, l, k):
            o = c.vec_off[(nm, l)] + k
            return vecs[:, o:o + 1]

        def dma(q, out, in_, reads, writes):
            eng = "sp" if q == "sp" else "pool"
            return P.add(eng, lambda e: e.dma_start(out=out, in_=in_), reads, writes, dma=True)

        wsub = {}

        def conv(dst, src, wkey, after=()):
            lst = wsub.setdefault(wkey, [])
            sk = wkey + ("#", len(lst))
            lst.append(sk)
            dma("pool", dst, src, tuple(after), (sk,))

        def WK(*wkey):
            return tuple(wsub[wkey])

        def mm(bank, lhsT, rhs, start, stop, reads, m=128, n=512):
            out = banks[bank][0:m, 0:n]
            P.add("pe", lambda e: e.matmul(out, lhsT=lhsT, rhs=rhs, start=start, stop=stop),
                  reads, bk(bank))

        def tr(bank, col, in_ap, reads):
            out = banks[bank][:, col:col + 128]
            P.add("pe", lambda e: e.transpose(out, in_ap, id32), tuple(reads) + r_id32.keys, bk(bank))

        bank_rr = {}

        def nb(cls, lst):
            i = bank_rr.get(cls, 0)
            bank_rr[cls] = i + 1
            return lst[i % len(lst)]

        dma("sp", vecs, vecs_in, (), r_vecs.keys)
        dma("sp", id32, gfin_in, (), r_id32.keys)
        P.add("dve", lambda e: e.memset(ones_bf, 1.0), (), r_ones.keys)
        ones32 = r_ones32.f32()
        P.add("dve", lambda e: e.memset(ones32, 1.0), (), r_ones32.keys)
        P.add("dve", lambda e: e.memset(r_eps.f32(), NORM_EPS), (), r_eps.keys)

        def jobs_ffn(l, f):
            jobs = []
            wg = w_gate[f][l].rearrange("(k p) n -> p k n", p=128)
            wu = w_up[f][l].rearrange("(k p) n -> p k n", p=128)
            wd = w_down[f][l].rearrange("(k p) n -> p k n", p=128)
            for g in range(NG_FF):
                for gu, w in ((0, wg), (1, wu)):
                    for b in range(2):
                        col = g * 256 + b * 128
                        jobs.append((WGU[l][f][g, :, gu, b, :, :], w[:, :, col:col + 128], ("WGU", l, f, g)))
            for g in range(NG_WD):
                for b in range(2):
                    col = g * 256 + b * 128
                    jobs.append((WD[l][f][g, :, b, :, :], wd[:, :, col:col + 128], ("WD", l, f, g)))
            return jobs

        def jobs_projA(l):
            jobs = []
            wi = w_in[l].rearrange("(k p) n -> p k n", p=128)
            for g in range(FW // 512):
                jobs.append((WUF[l][g], wi[:, :, g * 512:(g + 1) * 512], ("WUF", l, g)))
            for b in range(KVC):
                col = c.c_ckv + b * 128
                jobs.append((WCKV[l][:, b, :, :], wi[:, :, col:col + 128], ("WCKV", l)))
            jobs.append((WKR[l][:, 0, :, :], wi[:, :, c.c_kr:c.c_kr + 64], ("WKR", l)))
            jobs.append((WKR[l][:, 1, :, 0:32], wi[:, :, c.c_kr + 32:c.c_kr + 64], ("WKR", l)))
            jobs.append((WKR[l][:, 1, :, 32:64], wi[:, :, c.c_kr:c.c_kr + 32], ("WKR", l)))
            wkv = w_ukv[l].rearrange("(k p) n -> p k n", p=128)
            for g in range(NH // 4):
                for hh in range(4):
                    h = g * 4 + hh
                    jobs.append((WUK[l][g, :, hh, :, :], wkv[:, :, h * 256:h * 256 + 128], ("WUK", l, g)))
                    jobs.append((WUV[l][g, :, :, hh * 128:(hh + 1) * 128],
                                 wkv[:, :, h * 256 + 128:h * 256 + 256], ("WUV", l, g)))
            return jobs

        def jobs_mix(l):
            jobs = []
            wi = w_in[l].rearrange("(k p) n -> p k n", p=128)
            for b in range(QC):
                col = c.c_cq + b * 128
                jobs.append((WCQ[l][:, b, :, :], wi[:, :, col:col + 128], ("WCQ", l)))
            for g in range(D // 256):
                for ab in range(2):
                    for b in range(2):
                        col = c.c_g + ab * D + g * 256 + b * 128
                        jobs.append((WGATE[l][g, :, ab, b, :, :], wi[:, :, col:col + 128], ("WGATE", l, g)))
            wo = w_out[l].rearrange("(k p) n -> p k n", p=128)
            wm = w_mla_o[l].rearrange("(k p) n -> p k n", p=128)
            wf = w_fourier[l].rearrange("(k p) n -> p k n", p=128)
            for g in range(D // 512):
                for b in range(4):
                    col = g * 512 + b * 128
                    jobs.append((WOUT[l][g, :, b, :, :], wo[:, :, col:col + 128], ("WOUT", l, g)))
                    jobs.append((WMO[l][g, :, b, :, :], wm[:, :, col:col + 128], ("WMO", l, g)))
                    jobs.append((WFO[l][g, :, b, :, :], wf[:, :, col:col + 128], ("WFO", l, g)))
            wq = w_uq[l].rearrange("(k p) n -> p k n", p=128)
            NQ = QC * 128
            for g in range(NH // 4):
                for hh in range(4):
                    h = g * 4 + hh
                    dst = WUQ[l][g, :, hh, :]
                    jobs.append((dst[:, 0:NQ].rearrange("p (k n) -> p k n", k=QC),
                                 wq[:, :, h * 192:h * 192 + 128], ("WUQ", l, g)))
                    r0 = dst[:, NQ:NQ + QC * 64].rearrange("p (k n) -> p k n", k=QC)
                    r1 = dst[:, NQ + QC * 64:NQ + 2 * QC * 64].rearrange("p (k n) -> p k n", k=QC)
                    jobs.append((r0, wq[:, :, h * 192 + 128:h * 192 + 192], ("WUQ", l, g)))
                    jobs.append((r1[:, :, 0:32], wq[:, :, h * 192 + 160:h * 192 + 192], ("WUQ", l, g)))
                    jobs.append((r1[:, :, 32:64], wq[:, :, h * 192 + 128:h * 192 + 160], ("WUQ", l, g)))
            return jobs

        def jobs_dft(tiles=None):
            jobs = []
            for t in (range(NT) if tiles is None else tiles):
                for ng in range(NKC // 4):
                    for cs, src in ((0, dft_c_in), (1, dft_s_in)):
                        s3 = src[ng * 512:(ng + 1) * 512, t * 512:(t + 1) * 512].rearrange("(j p) n -> p j n", p=128)
                        jobs.append((DFT[t, ng, :, :, cs, :], s3, ("DFT", t, ng)))
            for cs in range(2):
                jobs.append((CDFT[:, cs, :, :], cdft_in[cs].rearrange("(k p) n -> p k n", p=128), ("CDFT",)))
            jobs.append((QF[:, :], flags_in[2:4, :], ("QF",)))
            for l in range(L):
                jobs.append((KR[l][64:66, :], flags_in[0:2, :], ("KRF", l)))
            return jobs

        def run_jobs(jobs, after=()):
            for dst, src, key in jobs:
                conv(dst, src, key, after)

        def spread(jobs, n):
            out, k = [], len(jobs)
            for i in range(n):
                out.append(jobs[i * k // n:(i + 1) * k // n])
            return out

        ffn_act = [r_R.sub(k * T * 2, T * 2) for k in range(FC)]
        o = FC * T * 2
        WGU_SLOT = 2 * 2 * KC * 128 * 2
        ffn_wgu = [r_R.sub(o + i * WGU_SLOT, WGU_SLOT) for i in range(2)]
        o += 2 * WGU_SLOT
        WD_SLOT = 2 * FC * 128 * 2
        ffn_wd = [r_R.sub(o + i * WD_SLOT, WD_SLOT) for i in range(2)]
        o += 2 * WD_SLOT
        assert o <= RSIZE, ("FFN view", o, RSIZE)
        WSLOT = max(4 * KC * 128 * 2, 4 * NH * 128 * 2, KC * 512 * 2)
        wslots = [r_R.sub(i * WSLOT, WSLOT) for i in range(2)]
        ws_rr = [0]

        def wslot():
            i = ws_rr[0]
            ws_rr[0] += 1
            return wslots[i % 2]

        V0 = 2 * WSLOT

        def norm(src, n, dim, gname, l, dst, dst_f32=False):
            bank = nb("stat", [7])
            for k in range(n):
                sq = r_sq[k % 4]
                s_ap, q_ap = src[k].f32(), sq.bf()
                P.add("dve", lambda e, s_ap=s_ap, q_ap=q_ap: e.tensor_tensor(out=q_ap, in0=s_ap, in1=s_ap, op=ALU.mult),
                      src[k].keys, sq.keys)
                mm(bank, ones_bf, q_ap, k == 0, k == n - 1, r_ones.keys + sq.keys)
            std, rstd = r_std.f32(), r_rstd.f32()
            bap = banks[bank][:, :]
            P.add("act", lambda e: e.activation(out=std, in_=bap, func=AF.Sqrt, bias=eps_ap, scale=1.0 / dim),
                  bk(bank) + r_eps.keys, r_std.keys)
            P.add("dve", lambda e: e.reciprocal(out=rstd, in_=std), r_std.keys, r_rstd.keys)
            for k in range(n):
                s_ap = src[k].f32()
                d_ap = dst[k].f32() if dst_f32 else dst[k].bf()
                g_ap = vcol(gname, l, k)
                P.add("dve", lambda e, s_ap=s_ap, d_ap=d_ap, g_ap=g_ap: e.scalar_tensor_tensor(
                    out=d_ap, in0=s_ap, scalar=g_ap, in1=rstd, op0=ALU.mult, op1=ALU.mult),
                    src[k].keys + r_rstd.keys + r_vecs.keys, dst[k].keys)

        def ffn(l, f):
            norm(xT, KC, D, "ffn1" if f == 0 else "ffn2", l, hT)
            hkeys = tuple(k for r in hT for k in r.keys)
            for g in range(NG_FF):
                slot = ffn_wgu[g % 2]
                dma("sp", slot.bf(), WGU[l][f][g].rearrange("p a b k n -> p (a b k n)"),
                    WK("WGU", l, f, g), slot.keys)
                w = slot.bf(2, 2, KC, 128)
                for b in range(2):
                    fb = g * 2 + b
                    bg, bu = nb("g", [0, 1]), nb("u", [2, 3])
                    for gu, bank in ((0, bg), (1, bu)):
                        for k in range(KC):
                            mm(bank, w[:, gu, b, k, :], hT[k].bf(), k == 0, k == KC - 1, slot.keys + hT[k].keys)
                    sg = r_sg[fb % 2]
                    sg_ap = sg.f32()
                    g_ap, u_ap, a_ap = banks[bg][:, :], banks[bu][:, :], ffn_act[fb].bf()
                    P.add("act", lambda e, sg_ap=sg_ap, g_ap=g_ap: e.activation(out=sg_ap, in_=g_ap, func=AF.Silu),
                          bk(bg), sg.keys)
                    P.add("dve", lambda e, sg_ap=sg_ap, u_ap=u_ap, a_ap=a_ap: e.tensor_tensor(
                        out=a_ap, in0=sg_ap, in1=u_ap, op=ALU.mult), sg.keys + bk(bu), ffn_act[fb].keys)
            akeys = tuple(k for r in ffn_act for k in r.keys)
            for g in range(NG_WD):
                slot = ffn_wd[g % 2]
                dma("sp", slot.bf(), WD[l][f][g].rearrange("p b k n -> p (b k n)"), WK("WD", l, f, g), slot.keys)
                w = slot.bf(2, FC, 128)
                for b in range(2):
                    db = g * 2 + b
                    bank = nb("d", [4, 5, 6])
                    for k in range(FC):
                        mm(bank, w[:, b, k, :], ffn_act[k].bf(), k == 0, k == FC - 1, slot.keys + ffn_act[k].keys)
                    x_ap, p_ap = xT[db].f32(), banks[bank][:, :]
                    P.add("dve", lambda e, x_ap=x_ap, p_ap=p_ap: e.scalar_tensor_tensor(
                        out=x_ap, in0=p_ap, scalar=0.5, in1=x_ap, op0=ALU.mult, op1=ALU.add),
                        bk(bank) + xT[db].keys, xT[db].keys)

        r_xpre = r_R.sub(RSIZE - KC * T * 4, KC * T * 4)

        def issue_x_input(t):
            src = x_in[t * T:(t + 1) * T, :].rearrange("(s p) d -> p s d", p=128)
            dma("sp", r_xpre.f32(NS, D), src, (), r_xpre.keys)

        def load_x_input(t):
            xtm = r_xpre
            xv = xtm.f32(NS, D)
            for k in range(KC):
                bank = nb("tr", [4, 5, 6])
                for s in range(NS):
                    tr(bank, s * 128, xv[:, s, k * 128:(k + 1) * 128], xtm.keys)
                d_ap, p_ap = xT[k].f32(), banks[bank][:, :]
                if k % 2 == 0:
                    P.add("act", lambda e, d_ap=d_ap, p_ap=p_ap: e.activation(out=d_ap, in_=p_ap, func=AF.Copy),
                          bk(bank), xT[k].keys)
                else:
                    P.add("dve", lambda e, d_ap=d_ap, p_ap=p_ap: e.tensor_copy(out=d_ap, in_=p_ap),
                          bk(bank), xT[k].keys)

        def prefetch_x(l, t):
            dma("sp", r_xpre.f32(), X1[l][t], (("X1", l, t),), r_xpre.keys)

        def load_x_scratch(l, t):
            for k in range(KC):
                s_ap, d_ap = r_xpre.sub(k * T * 4, T * 4).f32(), xT[k].f32()
                kk = r_xpre.sub(k * T * 4, T * 4).keys
                if k % 2 == 0:
                    P.add("act", lambda e, d_ap=d_ap, s_ap=s_ap: e.activation(out=d_ap, in_=s_ap, func=AF.Copy),
                          kk, xT[k].keys)
                else:
                    P.add("dve", lambda e, d_ap=d_ap, s_ap=s_ap: e.tensor_copy(out=d_ap, in_=s_ap), kk, xT[k].keys)

        def store_x_scratch(l, t):
            dma("sp", X1[l][t], r_xT.f32(), r_xT.keys, (("X1", l, t),))

        def load_rope(t):
            rv = r_rope.f32(2, T)
            dma("sp", rv[0:64, 0, :], rope_c_in[:, t * T:(t + 1) * T], (), r_rope.keys)
            dma("sp", rv[0:64, 1, :], rope_s_in[:, t * T:(t + 1) * T], (), r_rope.keys)

        def rope_combine(bx, bs, dst_ap, dst_keys):
            rv = r_rope.f32(2, T)
            t0, t1 = r_tmp[0].f32()[0:64, :], r_tmp[1].f32()[0:64, :]
            xa, xs = banks[bx][0:64, :], banks[bs][0:64, :]
            P.add("dve", lambda e: e.tensor_tensor(out=t0, in0=rv[0:64, 0, :], in1=xa, op=ALU.mult),
                  bk(bx) + r_rope.keys, r_tmp[0].keys)
            P.add("dve", lambda e: e.tensor_tensor(out=t1, in0=rv[0:64, 1, :], in1=xs, op=ALU.mult),
                  bk(bs) + r_rope.keys, r_tmp[1].keys)
            P.add("dve", lambda e: e.tensor_tensor(out=dst_ap, in0=t0, in1=t1, op=ALU.add),
                  r_tmp[0].keys + r_tmp[1].keys, dst_keys)

        def stage_p(l, t):
            norm(xT, KC, D, "mix", l, hT)
            o = V0
            p_ckvT = [r_R.sub(o + k * T * 4, T * 4) for k in range(KVC)]
            o += KVC * T * 4
            p_ckvn = [r_R.sub(o + k * T * 2, T * 2) for k in range(KVC)]
            o += KVC * T * 2
            p_uf = r_R.sub(o, NS * FW * 2)
            o += NS * FW * 2
            p_kt = r_R.sub(o, NH * T * 2)
            o += NH * T * 2
            p_v = r_R.sub(o, NS * NH * 128 * 2)
            o += NS * NH * 128 * 2
            p_kr = r_R.sub(o, T * 2)
            o += T * 2
            assert o <= RSIZE, ("P view", o, RSIZE)
            load_rope(t)
            ufv = p_uf.bf(NS, FW)
            for g in range(FW // 512):
                slot = wslot()
                dma("sp", slot.sub(0, KC * 512 * 2).bf(), WUF[l][g].rearrange("p k n -> p (k n)"),
                    WK("WUF", l, g), slot.keys)
                w = slot.sub(0, KC * 512 * 2).bf(KC, 512)
                for s in range(NS):
                    bank = nb("p", [0, 1, 2, 3])
                    for k in range(KC):
                        mm(bank, hT[k].bf()[:, s * 128:(s + 1) * 128], w[:, k, :], k == 0, k == KC - 1,
                           slot.keys + hT[k].keys)
                    d_ap, p_ap = ufv[:, s, g * 512:(g + 1) * 512], banks[bank][:, :]
                    if s % 2 == 0:
                        P.add("act", lambda e, d_ap=d_ap, p_ap=p_ap: e.activation(out=d_ap, in_=p_ap, func=AF.Copy),
                              bk(bank), p_uf.keys)
                    else:
                        P.add("dve", lambda e, d_ap=d_ap, p_ap=p_ap: e.tensor_copy(out=d_ap, in_=p_ap),
                              bk(bank), p_uf.keys)
            for cb in range(FWC):
                dma("sp", UF[l][cb, :, t * NS:(t + 1) * NS, :], ufv[:, :, cb * 128:(cb + 1) * 128],
                    p_uf.keys, (("UF", l, t, cb),))
            slot = wslot()
            dma("sp", slot.sub(0, KVC * KC * 128 * 2).bf(), WCKV[l].rearrange("p b k n -> p (b k n)"),
                WK("WCKV", l), slot.keys)
            w = slot.sub(0, KVC * KC * 128 * 2).bf(KVC, KC, 128)
            for b in range(KVC):
                bank = nb("p", [0, 1, 2, 3])
                for k in range(KC):
                    mm(bank, w[:, b, k, :], hT[k].bf(), k == 0, k == KC - 1, slot.keys + hT[k].keys)
                d_ap, p_ap = p_ckvT[b].f32(), banks[bank][:, :]
                P.add("act", lambda e, d_ap=d_ap, p_ap=p_ap: e.activation(out=d_ap, in_=p_ap, func=AF.Copy),
                      bk(bank), p_ckvT[b].keys)
            slot = wslot()
            dma("sp", slot.sub(0, 2 * KC * 64 * 2).bf(), WKR[l].rearrange("p a k n -> p (a k n)"),
                WK("WKR", l), slot.keys)
            w = slot.sub(0, 2 * KC * 64 * 2).bf(2, KC, 64)
            bx, bs = nb("p", [0, 1, 2, 3]), nb("p", [0, 1, 2, 3])
            for a, bank in ((0, bx), (1, bs)):
                for k in range(KC):
                    mm(bank, w[:, a, k, :], hT[k].bf(), k == 0, k == KC - 1, slot.keys + hT[k].keys, m=64)
            rope_combine(bx, bs, p_kr.bf()[0:64, :], p_kr.keys)
            dma("sp", KR[l][0:64, t * T:(t + 1) * T], p_kr.bf()[0:64, :], p_kr.keys, (("KR", l, t),))
            norm(p_ckvT, KVC, KVL, "kva", l, p_ckvn)
            ckeys = tuple(k for r in p_ckvn for k in r.keys)
            ktv = p_kt.bf(NH, T)
            vv = p_v.bf(NS, NH * 128)
            KVS = 4 * KVC * 128 * 2 + KVC * 512 * 2
            nkvs = max(1, min(4, (2 * WSLOT) // KVS))
            for g in range(NH // 4):
                slot = r_R.sub((g % nkvs) * KVS, KVS)
                sk = slot.sub(0, 4 * KVC * 128 * 2)
                sv = slot.sub(4 * KVC * 128 * 2, KVC * 512 * 2)
                dma("sp", sk.bf(), WUK[l][g].rearrange("p h k n -> p (h k n)"), WK("WUK", l, g), sk.keys)
                dma("sp", sv.bf(), WUV[l][g].rearrange("p k n -> p (k n)"), WK("WUV", l, g), sv.keys)
                wk, wv = sk.bf(4, KVC, 128), sv.bf(KVC, 512)
                for hh in range(4):
                    h = g * 4 + hh
                    bank = nb("p", [0, 1, 2, 3])
                    for k in range(KVC):
                        mm(bank, wk[:, hh, k, :], p_ckvn[k].bf(), k == 0, k == KVC - 1, sk.keys + p_ckvn[k].keys)
                    d_ap, p_ap = ktv[:, h, :], banks[bank][:, :]
                    P.add("act", lambda e, d_ap=d_ap, p_ap=p_ap: e.activation(out=d_ap, in_=p_ap, func=AF.Copy),
                          bk(bank), p_kt.keys)
                for s in range(NS):
                    bank = nb("p", [0, 1, 2, 3])
                    for k in range(KVC):
                        mm(bank, p_ckvn[k].bf()[:, s * 128:(s + 1) * 128], wv[:, k, :], k == 0, k == KVC - 1,
                           sv.keys + p_ckvn[k].keys)
                    d_ap, p_ap = vv[:, s, g * 512:(g + 1) * 512], banks[bank][:, :]
                    P.add("dve", lambda e, d_ap=d_ap, p_ap=p_ap: e.tensor_copy(out=d_ap, in_=p_ap),
                          bk(bank), p_v.keys)
            dma("sp", KT[l][:, :, t * T:(t + 1) * T].rearrange("h p n -> p h n"), ktv, p_kt.keys, (("KT", l, t),))
            for h in range(NH):
                dma("sp", VS[l][h, :, t * NS:(t + 1) * NS, :], vv[:, :, h * 128:(h + 1) * 128],
                    p_v.keys, (("VS", l, t, h),))

        def stage_m(l, t):
            o = V0
            m_OT = [r_R.sub(o + h * T * 2, T * 2) for h in range(NH)]
            o += NH * T * 2
            m_MT = [r_R.sub(o + k * T * 2, T * 2) for k in range(FWC)]
            o += FWC * T * 2
            PH = o
            all_uf = tuple(("UF", l, tt, cb) for tt in range(NT) for cb in range(FWC))
            all_kt = tuple(("KT", l, tt) for tt in range(NT))
            all_vs = tuple(("VS", l, tt, hh) for tt in range(NT) for hh in range(NH))
            all_kr = tuple(("KR", l, tt) for tt in range(NT)) + WK("KRF", l)
            o = PH
            DSLOT = 4 * 2 * 512 * 2
            NDS = 4
            d_dft = [r_R.sub(o + i * DSLOT, DSLOT) for i in range(NDS)]
            o += NDS * DSLOT
            USLOT = NKC * 128 * 2
            d_uf = [r_R.sub(o + i * USLOT, USLOT) for i in range(4)]
            o += 4 * USLOT
            if 2 * FWC <= NH:
                d_A = [r_R.sub(V0 + k * T * 2, T * 2) for k in range(FWC)]
                d_B = [r_R.sub(V0 + (FWC + k) * T * 2, T * 2) for k in range(FWC)]
            else:
                d_A = [r_R.sub(o + k * T * 2, T * 2) for k in range(FWC)]
                o += FWC * T * 2
                d_B = [r_R.sub(o + k * T * 2, T * 2) for k in range(FWC)]
                o += FWC * T * 2
            assert o <= RSIZE, ("DFT view", o, RSIZE)
            di = 0
            for ps in range(FWC // 4):
                uvs = []
                for ci in range(4):
                    cb = ps * 4 + ci
                    us = d_uf[ci]
                    dma("sp", us.bf(), UF[l][cb].rearrange("p j n -> p (j n)"), all_uf, us.keys)
                    uvs.append((us, us.bf(NKC, 128)))
                for ng in range(NKC // 4):
                    ds = d_dft[di % NDS]
                    di += 1
                    dma("sp", ds.bf(), DFT[t, ng].rearrange("p j c n -> p (j c n)"), WK("DFT", t, ng), ds.keys)
                    dv = ds.bf(4, 2, 512)
                    for j in range(4):
                        kc = ng * 4 + j
                        for ci in range(4):
                            us, uv = uvs[ci]
                            mm(ci, uv[:, kc, :], dv[:, j, 0, :], kc == 0, kc == NKC - 1, us.keys + ds.keys)
                            mm(4 + ci, uv[:, kc, :], dv[:, j, 1, :], kc == 0, kc == NKC - 1, us.keys + ds.keys)
                for ci in range(4):
                    cb = ps * 4 + ci
                    a_ap, b_ap = d_A[cb].bf(), d_B[cb].bf()
                    pa, pb = banks[ci][:, :], banks[4 + ci][:, :]
                    P.add("act", lambda e, a_ap=a_ap, pa=pa: e.activation(out=a_ap, in_=pa, func=AF.Copy),
                          bk(ci), d_A[cb].keys)
                    P.add("dve", lambda e, b_ap=b_ap, pb=pb: e.tensor_copy(out=b_ap, in_=pb),
                          bk(4 + ci), d_B[cb].keys)
            cd = r_cdft.bf(2, 2, GD)
            for grp in range(FW // GD):
                for jb in range(2):
                    bank = nb("p", [4, 5, 6])
                    n_acc = 0
                    for cs, src in ((0, d_A), (1, d_B)):
                        for kk in range(2):
                            mm(bank, cd[:, cs, kk, jb * 128:(jb + 1) * 128], src[grp * 2 + kk].bf(),
                               n_acc == 0, n_acc == 3, r_cdft.keys + src[grp * 2 + kk].keys)
                            n_acc += 1
                    d_ap, p_ap = m_MT[grp * 2 + jb].bf(), banks[bank][:, :]
                    P.add("dve", lambda e, d_ap=d_ap, p_ap=p_ap: e.tensor_copy(out=d_ap, in_=p_ap),
                          bk(bank), m_MT[grp * 2 + jb].keys)
            norm(xT, KC, D, "mix", l, hT)
            load_rope(t)
            o = PH
            a_cqT = [r_R.sub(o + k * T * 4, T * 4) for k in range(QC)]
            o += QC * T * 4
            a_cqn = [r_R.sub(o + k * T * 2, T * 2) for k in range(QC)]
            o += QC * T * 2
            a_qn = [r_R.sub(o + i * T * 2, T * 2) for i in range(2)]
            o += 2 * T * 2
            a_qr = [r_R.sub(o + i * T * 2, T * 2) for i in range(2)]
            o += 2 * T * 2
            a_kr = r_R.sub(o, NTOK * 2)
            o += NTOK * 2
            a_kt = [r_R.sub(o + i * NTOK * 2, NTOK * 2) for i in range(2)]
            o += 2 * NTOK * 2
            a_v = [r_R.sub(o + i * NTOK * 2, NTOK * 2) for i in range(2)]
            o += 2 * NTOK * 2
            NPT = 4
            a_pt = [r_R.sub(o + i * T * 2, T * 2) for i in range(NPT)]
            o += NPT * T * 2
            a_rinv = r_R.sub(o, T * 4)
            o += T * 4
            a_acc = [r_R.sub(o + i * T * 4, T * 4) for i in range(2)]
            o += 2 * T * 4
            assert o <= RSIZE, ("attention view", o, RSIZE)
            slot = wslot()
            dma("sp", slot.sub(0, QC * KC * 128 * 2).bf(), WCQ[l].rearrange("p b k n -> p (b k n)"),
                WK("WCQ", l), slot.keys)
            w = slot.sub(0, QC * KC * 128 * 2).bf(QC, KC, 128)
            for b in range(QC):
                bank = nb("p", [4, 5, 6])
                for k in range(KC):
                    mm(bank, w[:, b, k, :], hT[k].bf(), k == 0, k == KC - 1, slot.keys + hT[k].keys)
                d_ap, p_ap = a_cqT[b].f32(), banks[bank][:, :]
                P.add("act", lambda e, d_ap=d_ap, p_ap=p_ap: e.activation(out=d_ap, in_=p_ap, func=AF.Copy),
                      bk(bank), a_cqT[b].keys)
            norm(a_cqT, QC, QL, "qa", l, a_cqn)
            krv = a_kr.bf()
            dma("sp", krv[0:66, :], KR[l][:, :], all_kr, a_kr.keys)
            for i in range(2):
                dma("sp", a_qr[i].bf()[64:66, :], QF[:, t * T:(t + 1) * T], WK("QF",), a_qr[i].keys)
            scale = float((128 + 64) ** -0.5)
            NQ = QC * 128
            rv = r_rope.f32(2, T)
            QB = 7
            wq_cur = {}

            def kv_load(h):
                dma("sp", a_kt[h % 2].bf(), KT[l][h], all_kt, a_kt[h % 2].keys)
                dma("sp", a_v[h % 2].bf(), VS[l][h].rearrange("p j n -> p (j n)"), all_vs, a_v[h % 2].keys)

            def wq_load(g):
                qslot = wslot()
                qs = qslot.sub(0, 4 * (NQ + 2 * QC * 64) * 2)
                dma("sp", qs.bf(), WUQ[l][g].rearrange("p h n -> p (h n)"), WK("WUQ", l, g), qs.keys)
                wq_cur[g] = (qs, qs.bf(4, NQ + 2 * QC * 64))

            def q_part(h, part):
                g, hh = divmod(h, 4)
                if g not in wq_cur:
                    wq_load(g)
                qs, wq = wq_cur[g]
                qn, qr = a_qn[h % 2], a_qr[h % 2]
                t0, t1 = r_tmp[0].f32()[0:64, :], r_tmp[1].f32()[0:64, :]
                if part == 0:
                    for k in range(QC):
                        mm(QB, wq[:, hh, k * 128:(k + 1) * 128], a_cqn[k].bf(), k == 0, k == QC - 1,
                           qs.keys + a_cqn[k].keys)
                    d_ap, p_ap = qn.bf(), banks[QB][:, :]
                    P.add("dve", lambda e, d_ap=d_ap, p_ap=p_ap: e.tensor_copy(out=d_ap, in_=p_ap), bk(QB), qn.keys)
                elif part == 1:
                    for k in range(QC):
                        mm(QB, wq[:, hh, NQ + k * 64:NQ + (k + 1) * 64], a_cqn[k].bf(), k == 0, k == QC - 1,
                           qs.keys + a_cqn[k].keys, m=64)
                    xa = banks[QB][0:64, :]
                    P.add("dve", lambda e: e.tensor_tensor(out=t0, in0=rv[0:64, 0, :], in1=xa, op=ALU.mult),
                          bk(QB) + r_rope.keys, r_tmp[0].keys)
                else:
                    for k in range(QC):
                        mm(QB, wq[:, hh, NQ + QC * 64 + k * 64:NQ + QC * 64 + (k + 1) * 64], a_cqn[k].bf(),
                           k == 0, k == QC - 1, qs.keys + a_cqn[k].keys, m=64)
                    xs = banks[QB][0:64, :]
                    d_ap = qr.bf()[0:64, :]
                    P.add("dve", lambda e: e.tensor_tensor(out=t1, in0=rv[0:64, 1, :], in1=xs, op=ALU.mult),
                          bk(QB) + r_rope.keys, r_tmp[1].keys)
                    P.add("dve", lambda e: e.tensor_tensor(out=d_ap, in0=t0, in1=t1, op=ALU.add),
                          r_tmp[0].keys + r_tmp[1].keys, qr.keys)

            items = [(h, kc) for h in range(NH) for kc in range(NKC)]
            NI = len(items)
            LOOK = 2

            def emit_st(i):
                h, kc = items[i]
                kts, qn, qr = a_kt[h % 2], a_qn[h % 2], a_qr[h % 2]
                bst = i % 3
                mm(bst, kts.bf()[:, kc * 128:(kc + 1) * 128], qn.bf(), True, False, kts.keys + qn.keys)
                mm(bst, krv[0:66, kc * 128:(kc + 1) * 128], qr.bf()[0:66, :], False, True, a_kr.keys + qr.keys)

            kv_load(0)
            if NH > 1:
                kv_load(1)
            for part in range(3):
                q_part(0, part)
            for i in range(min(LOOK, NI)):
                emit_st(i)
            qpts = {1: 0, NKC // 4 + 1: 1, NKC // 2 + 1: 2}
            assert NKC // 2 + 1 < NKC - LOOK
            for i in range(NI):
                h, kc = items[i]
                if i + LOOK < NI:
                    emit_st(i + LOOK)
                bst, bo, br = i % 3, 3 + h % 2, 5 + h % 2
                vs = a_v[h % 2]
                vsv = vs.bf(NKC, 128)
                pt = a_pt[i % NPT]
                pt_ap, st_ap = pt.bf(), banks[bst][:, :]
                P.add("act", lambda e, pt_ap=pt_ap, st_ap=st_ap: e.activation(
                    out=pt_ap, in_=st_ap, func=AF.Exp, scale=scale), bk(bst), pt.keys)
                mm(bo, vsv[:, kc, :], pt_ap, kc == 0, kc == NKC - 1, vs.keys + pt.keys)
                acc = a_acc[h % 2]
                acc_ap = acc.f32()
                if kc % 3 == 0:
                    mm(br, ones_bf, pt_ap, kc == 0, False, r_ones.keys + pt.keys)
                elif kc == 1:
                    P.add("dve", lambda e, acc_ap=acc_ap, pt_ap=pt_ap: e.tensor_copy(out=acc_ap, in_=pt_ap),
                          pt.keys, acc.keys)
                else:
                    P.add("dve", lambda e, acc_ap=acc_ap, pt_ap=pt_ap: e.tensor_tensor(
                        out=acc_ap, in0=acc_ap, in1=pt_ap, op=ALU.add), pt.keys + acc.keys, acc.keys)
                if kc == NKC - 1:
                    mm(br, ones32, acc_ap, False, True, r_ones32.keys + acc.keys)
                if kc in qpts and h + 1 < NH:
                    q_part(h + 1, qpts[kc])
                if kc == NKC - 1:
                    rinv = a_rinv.f32()
                    pr, po = banks[br][:, :], banks[bo][:, :]
                    o_ap = m_OT[h].bf()
                    P.add("dve", lambda e, pr=pr, rinv=rinv: e.reciprocal(out=rinv, in_=pr), bk(br), a_rinv.keys)
                    P.add("dve", lambda e, po=po, o_ap=o_ap, rinv=rinv: e.tensor_tensor(
                        out=o_ap, in0=rinv, in1=po, op=ALU.mult), bk(bo) + a_rinv.keys, m_OT[h].keys)
                    if h + 2 < NH:
                        kv_load(h + 2)
                    if h % 4 == 1 and h // 4 + 1 < NH // 4:
                        wq_load(h // 4 + 1)
            o = PH
            g_mT = [r_R.sub(o + k * T * 2, T * 2) for k in range(KC)]
            o += KC * T * 2
            assert o <= RSIZE
            for gI in range(D // 256):
                if gI % 2 == 0:
                    so, sm, sf = wslot(), wslot(), None
                    s_mo = so.sub(0, 4 * NH * 128 * 2)
                    dma("sp", s_mo.bf(), WMO[l][gI // 2].rearrange("p b k n -> p (b k n)"),
                        WK("WMO", l, gI // 2), s_mo.keys)
                    wmo = s_mo.bf(4, NH, 128)
                    s_fo = sm.sub(0, 4 * FWC * 128 * 2)
                    dma("sp", s_fo.bf(), WFO[l][gI // 2].rearrange("p b k n -> p (b k n)"),
                        WK("WFO", l, gI // 2), s_fo.keys)
                    wfo = s_fo.bf(4, FWC, 128)
                gslot = r_R.sub(RSIZE - 2 * 2 * KC * 128 * 2 * (1 + gI % 2), 2 * 2 * KC * 128 * 2)
                assert RSIZE - 2 * (2 * 2 * KC * 128 * 2) >= o
                dma("sp", gslot.bf(), WGATE[l][gI].rearrange("p a b k n -> p (a b k n)"), WK("WGATE", l, gI), gslot.keys)
                wg = gslot.bf(2, 2, KC, 128)
                for b in range(2):
                    db = gI * 2 + b
                    b4 = db % 4
                    bya, byb, bga, bgb = nb("ya", [0, 1]), nb("yb", [2, 3]), nb("ga", [4, 5]), nb("gb", [6, 7])
                    for k in range(FWC):
                        mm(bya, wfo[:, b4, k, :], m_MT[k].bf(), k == 0, k == FWC - 1, s_fo.keys + m_MT[k].keys)
                    for k in range(NH):
                        mm(byb, wmo[:, b4, k, :], m_OT[k].bf(), k == 0, k == NH - 1, s_mo.keys + m_OT[k].keys)
                    for ab, bank in ((0, bga), (1, bgb)):
                        for k in range(KC):
                            mm(bank, wg[:, ab, b, k, :], hT[k].bf(), k == 0, k == KC - 1, gslot.keys + hT[k].keys)
                    sa, sb_ = r_sg[0], r_sg[1]
                    sa_ap, sb_ap = sa.f32(), sb_.f32()
                    pga, pgb, pya, pyb = banks[bga][:, :], banks[bgb][:, :], banks[bya][:, :], banks[byb][:, :]
                    ba_ap, bb_ap = vcol("bg", l, db), vcol("bg", l, KC + db)
                    P.add("act", lambda e, sa_ap=sa_ap, pga=pga, ba_ap=ba_ap: e.activation(
                        out=sa_ap, in_=pga, func=AF.Sigmoid, bias=ba_ap), bk(bga) + r_vecs.keys, sa.keys)
                    P.add("act", lambda e, sb_ap=sb_ap, pgb=pgb, bb_ap=bb_ap: e.activation(
                        out=sb_ap, in_=pgb, func=AF.Sigmoid, bias=bb_ap), bk(bgb) + r_vecs.keys, sb_.keys)
                    t0, t1 = r_tmp[0].f32(), r_tmp[1].f32()
                    m_ap = g_mT[db].bf()
                    P.add("dve", lambda e, sa_ap=sa_ap, pya=pya, t0=t0: e.tensor_tensor(
                        out=t0, in0=sa_ap, in1=pya, op=ALU.mult), sa.keys + bk(bya), r_tmp[0].keys)
                    P.add("dve", lambda e, sb_ap=sb_ap, pyb=pyb, t1=t1: e.tensor_tensor(
                        out=t1, in0=sb_ap, in1=pyb, op=ALU.mult), sb_.keys + bk(byb), r_tmp[1].keys)
                    P.add("dve", lambda e, t0=t0, t1=t1, m_ap=m_ap: e.tensor_tensor(
                        out=m_ap, in0=t0, in1=t1, op=ALU.add), r_tmp[0].keys + r_tmp[1].keys, g_mT[db].keys)
            for g in range(D // 512):
                slot = wslot()
                s_o = slot.sub(0, 4 * KC * 128 * 2)
                dma("sp", s_o.bf(), WOUT[l][g].rearrange("p b k n -> p (b k n)"), WK("WOUT", l, g), s_o.keys)
                w = s_o.bf(4, KC, 128)
                for b in range(4):
                    db = g * 4 + b
                    bank = nb("o", [4, 5, 6])
                    for k in range(KC):
                        mm(bank, w[:, b, k, :], g_mT[k].bf(), k == 0, k == KC - 1, s_o.keys + g_mT[k].keys)
                    x_ap, p_ap = xT[db].f32(), banks[bank][:, :]
                    P.add("dve", lambda e, x_ap=x_ap, p_ap=p_ap: e.tensor_tensor(
                        out=x_ap, in0=x_ap, in1=p_ap, op=ALU.add), bk(bank) + xT[db].keys, xT[db].keys)

        def final_out(t):
            yT = [r_R.sub(k * T * 4, T * 4) for k in range(KC)]
            ytm = r_R.sub(KC * T * 4, NS * D * 4)
            assert KC * T * 4 + NS * D * 4 <= RSIZE
            norm(xT, KC, D, "final", 0, yT, dst_f32=True)
            yv = ytm.f32(NS, D)
            for s in range(NS):
                for k4 in range(KC // 4):
                    bank = nb("tr", [4, 5, 6])
                    for j in range(4):
                        k = k4 * 4 + j
                        tr(bank, j * 128, yT[k].f32()[:, s * 128:(s + 1) * 128], yT[k].keys)
                    d_ap, p_ap = yv[:, s, k4 * 512:(k4 + 1) * 512], banks[bank][:, :]
                    if k4 % 2 == 0:
                        P.add("act", lambda e, d_ap=d_ap, p_ap=p_ap: e.activation(out=d_ap, in_=p_ap, func=AF.Copy),
                              bk(bank), ytm.keys)
                    else:
                        P.add("dve", lambda e, d_ap=d_ap, p_ap=p_ap: e.tensor_copy(out=d_ap, in_=p_ap),
                              bk(bank), ytm.keys)
            dst = y_out[t * T:(t + 1) * T, :].rearrange("(s p) d -> p s d", p=128)
            dma("sp", dst, yv, ytm.keys, (("Y", t),))

        run_jobs(jobs_dft()[-(3 + L):])
        dma("sp", r_cdft.bf(), CDFT.rearrange("p a k n -> p (a k n)"), WK("CDFT",), r_cdft.keys)
        run_jobs(jobs_ffn(0, 0))
        run_jobs(jobs_projA(0))
        NSMALL = 3 + L
        later0 = spread(jobs_dft([0])[:-NSMALL] + jobs_mix(0) + jobs_ffn(0, 1), NT)
        issue_x_input(0)
        for t in range(NT):
            load_x_input(t)
            ffn(0, 0)
            if t + 1 < NT:
                issue_x_input(t + 1)
            else:
                prefetch_x(0, 0)
            stage_p(0, t)
            store_x_scratch(0, t)
            run_jobs(later0[t])
        for l in range(L):
            last = (l == L - 1)
            if not last:
                later = spread(jobs_mix(l + 1) + jobs_ffn(l + 1, 1), NT)
            for t in range(NT):
                load_x_scratch(l, t)
                gate = xT[0].keys
                if not last and t == 0:
                    run_jobs(jobs_ffn(l + 1, 0) + jobs_projA(l + 1), after=gate)
                if l == 0 and t + 1 < NT:
                    run_jobs(jobs_dft([t + 1])[:-NSMALL], after=gate)
                stage_m(l, t)
                ffn(l, 1)
                if last:
                    if t + 1 < NT:
                        prefetch_x(l, t + 1)
                    final_out(t)
                else:
                    ffn(l + 1, 0)
                    if t + 1 < NT:
                        prefetch_x(l, t + 1)
                    else:
                        prefetch_x(l + 1, 0)
                    stage_p(l + 1, t)
                    store_x_scratch(l + 1, t)
                    run_jobs(later[t])

        P.emit(nc, block, sems, dsems)
    return nc


def _tables(cfg, seq_len):
    c = cfg
    nseq = c.NTOK // seq_len
    half = 32
    inv_freq = (1.0 / (ROPE_THETA ** (np.arange(half, dtype=np.float32) / half))).astype(np.float32)
    pos = (np.arange(c.NTOK) % seq_len).astype(np.float32)
    ang = (pos[None, :] * inv_freq[:, None]).astype(np.float32)
    cos, sin = np.cos(ang).astype(np.float32), np.sin(ang).astype(np.float32)
    rope_c = np.concatenate([cos, cos], 0)
    rope_s = np.concatenate([-sin, sin], 0)
    seq_id = np.arange(c.NTOK) // seq_len
    flags = np.zeros((4, c.NTOK), np.float32)
    a = (seq_id % 2 == 0).astype(np.float32)
    b = 1.0 - a
    if nseq == 1:
        b[:] = 0.0
    flags[0], flags[1] = -b, -a
    flags[2], flags[3] = BIGMASK * a, BIGMASK * b
    assert nseq <= 2
    n = np.arange(seq_len, dtype=np.float64)
    kn = np.outer(n, n) % seq_len
    angd = 2.0 * np.pi * kn / seq_len
    cs = (np.cos(angd) / np.sqrt(seq_len)).astype(np.float32)
    ss = (np.sin(angd) / np.sqrt(seq_len)).astype(np.float32)
    dft_c = np.zeros((c.NTOK, c.NTOK), np.float32)
    dft_s = np.zeros((c.NTOK, c.NTOK), np.float32)
    for i in range(nseq):
        sl = slice(i * seq_len, (i + 1) * seq_len)
        dft_c[sl, sl] = cs
        dft_s[sl, sl] = ss
    j = np.arange(c.GD, dtype=np.float64)
    angc = 2.0 * np.pi * (np.outer(j, j) % c.GD) / c.GD
    cdft = np.stack([np.cos(angc) / np.sqrt(c.GD), -np.sin(angc) / np.sqrt(c.GD)]).astype(np.float32)
    return dict(rope_c=rope_c, rope_s=rope_s, flags=flags, dft_c=dft_c, dft_s=dft_s, cdft=cdft)


def _vecs(cfg, inp):
    c = cfg
    v = np.zeros((128, c.NVEC), np.float32)

    def put(nm, l, arr):
        o = c.vec_off[(nm, l)]
        n = arr.shape[0] // 128
        v[:, o:o + n] = arr.reshape(n, 128).T
    for l in range(c.L):
        put("ffn1", l, inp["ffn1_norm"][l])
        put("mix", l, inp["mix_norm"][l])
        put("ffn2", l, inp["ffn2_norm"][l])
        put("qa", l, inp["q_a_norm"][l])
        put("kva", l, inp["kv_a_norm"][l])
        put("bg", l, inp["b_gate"][l])
    put("final", 0, inp["final_norm"])
    return v


_WNAMES = ("ffn1_w_gate", "ffn1_w_up", "ffn1_w_down", "ffn2_w_gate", "ffn2_w_up", "ffn2_w_down",
           "w_in", "w_uq", "w_ukv", "w_fourier", "w_mla_o", "w_out")


def run_cores(cfg, inp, xs, seq_lens):
    nc = build(cfg)
    shared = {k: np.ascontiguousarray(np.asarray(inp[k], dtype=np.float32)) for k in _WNAMES}
    shared["vecs"] = _vecs(cfg, {k: np.asarray(v) for k, v in inp.items()})
    shared["ident32"] = np.eye(128, dtype=np.float32)
    tabs = {}
    in_maps = []
    for x, sl in zip(xs, seq_lens):
        if sl not in tabs:
            tabs[sl] = _tables(cfg, sl)
        m = dict(shared)
        m.update(tabs[sl])
        m["x"] = np.ascontiguousarray(x, dtype=np.float32)
        in_maps.append(m)
    res = run_bass_kernel_spmd(nc, in_maps, core_ids=list(range(len(xs))))
    return [np.asarray(r["y"], dtype=np.float32) for r in res.results]


def kernel(**inputs):
    cfg = Cfg()
    xp = np.asarray(inputs["x_prompt"], dtype=np.float32)
    xsm = np.asarray(inputs["x_sample"], dtype=np.float32)
    B, S, D = xp.shape
    DB, DS, _ = xsm.shape
    xs = [xp[i] for i in range(B)] + [xsm[2 * j:2 * j + 2].reshape(2 * DS, D) for j in range(DB // 2)]
    seq_lens = [S] * B + [DS] * (DB // 2)
    outs = run_cores(cfg, inputs, xs, seq_lens)
    y_prompt = np.stack(outs[:B], 0)
    y_sample = np.concatenate([o.reshape(2, DS, D) for o in outs[B:]], 0)
    return (y_prompt, y_sample)
```
